# Optimizing a Trainium2 kernel written in Bass

```python
import math
import jax, jax.numpy as jnp
from jax import lax
import numpy as np

D_MODEL = 1024
BATCH = 2
SEQ = 16384
DEPTH = 1
DEC_BATCH = 8
DEC_SEQ = 16
PAST_LEN = 4096

CHUNK = 64
D_FF = 2816
A_WIDTH = 512
A_HEADS = 4
A_HEAD_DIM = A_WIDTH // A_HEADS
A_CHUNK = 128
B_WIDTH = 512
B_GROUP = 16
B_GROUPS = B_WIDTH // B_GROUP
B_STATE = 64
D_MIX = A_WIDTH + B_WIDTH
D_IN = 2 * A_WIDTH + B_WIDTH
EPS = 1e-6
DT_MIN = 1e-3
DT_MAX = 1e-1

kernel_name = 'hymba_style_gmlp_s5_macaron_stream'


def _rmsnorm(x, g):
    xf = x.astype(jnp.float32)
    y = xf * lax.rsqrt(jnp.mean(xf * xf, axis=-1, keepdims=True) + EPS)
    return (y * g.astype(jnp.float32)).astype(x.dtype)


def _swiglu(x, w_gate, w_up, w_down):
    return (jax.nn.silu(x @ w_gate) * (x @ w_up)) @ w_down


def _gmlp_mixer(z, g_v, w_s, b_s):
    bsz, length, _ = z.shape
    z = jax.nn.gelu(z)
    u, v = jnp.split(z, 2, axis=-1)
    v = _rmsnorm(v.reshape(bsz, length, A_HEADS, A_HEAD_DIM), g_v.reshape(A_HEADS, A_HEAD_DIM))
    c = min(length, A_CHUNK)
    mask = jnp.tril(jnp.ones((c, c), dtype=bool))
    ws = jnp.where(mask, w_s[:, :c, :c], 0)
    vc = v.reshape(bsz, length // c, c, A_HEADS, A_HEAD_DIM)
    s = jnp.einsum('hts,bnshd->bnthd', ws, vc) + b_s[:, :c].T[None, None, :, :, None]
    out = u * s.reshape(bsz, length, A_WIDTH)
    return out, v.reshape(bsz, length, A_WIDTH)


def _s5_mixer(xb, s0, lam_re, lam_im, log_dt, b_re, b_im, c_re, c_im, d_skip, w_glu, b_glu):
    bsz, length, _ = xb.shape
    f32 = jnp.float32
    lam = lax.complex(lam_re.astype(f32), lam_im.astype(f32))
    dt = jnp.exp(log_dt.astype(f32))[:, None]
    lam_bar = jnp.exp(lam * dt)
    b_bar = ((lam_bar - 1.0) / lam)[:, :, None] * lax.complex(b_re.astype(f32), b_im.astype(f32))
    c_mat = lax.complex(c_re.astype(f32), c_im.astype(f32))
    u = xb.astype(f32).reshape(bsz, length, B_GROUPS, B_GROUP)
    bu = jnp.einsum('gpn,blgn->blgp', b_bar, u.astype(jnp.complex64))
    bu = bu.at[:, 0].add(lam_bar[None] * s0)
    a = jnp.broadcast_to(lam_bar, bu.shape)

    def combine(left, right):
        a_l, b_l = left
        a_r, b_r = right
        return a_l * a_r, a_r * b_l + b_r

    _, states = lax.associative_scan(combine, (a, bu), axis=1)
    y = jnp.einsum('gnp,blgp->blgn', c_mat, states).real + d_skip.astype(f32).reshape(B_GROUPS, B_GROUP) * u
    y = jax.nn.gelu(y.reshape(bsz, length, B_WIDTH))
    y = y * jax.nn.sigmoid(y @ w_glu.astype(f32) + b_glu.astype(f32))
    return y.astype(xb.dtype), states[:, -1]


def _layer(x, s0, g_ffn1_pre, w_ffn1_gate, w_ffn1_up, w_ffn1_down, g_ffn1_post,
           g_mix_pre, w_in, gmlp_g_v, gmlp_w_s, gmlp_b_s,
           s5_lam_re, s5_lam_im, s5_log_dt, s5_b_re, s5_b_im, s5_c_re, s5_c_im, s5_d,
           s5_w_glu, s5_b_glu, g_a_out, g_b_out, w_out, g_mix_post,
           g_ffn2_pre, w_ffn2_gate, w_ffn2_up, w_ffn2_down, g_ffn2_post):
    h = x + 0.5 * _rmsnorm(_swiglu(_rmsnorm(x, g_ffn1_pre), w_ffn1_gate, w_ffn1_up, w_ffn1_down), g_ffn1_post)
    proj = _rmsnorm(h, g_mix_pre) @ w_in
    a_out, v_rows = _gmlp_mixer(proj[..., :2 * A_WIDTH], gmlp_g_v, gmlp_w_s, gmlp_b_s)
    b_out, s_last = _s5_mixer(proj[..., 2 * A_WIDTH:], s0, s5_lam_re, s5_lam_im, s5_log_dt,
                              s5_b_re, s5_b_im, s5_c_re, s5_c_im, s5_d, s5_w_glu, s5_b_glu)
    mixed = jnp.concatenate([_rmsnorm(a_out, g_a_out), _rmsnorm(b_out, g_b_out)], axis=-1) @ w_out
    h = h + _rmsnorm(mixed, g_mix_post)
    h = h + 0.5 * _rmsnorm(_swiglu(_rmsnorm(h, g_ffn2_pre), w_ffn2_gate, w_ffn2_up, w_ffn2_down), g_ffn2_post)
    return h, v_rows, s_last


def setup_inputs(seed: int = 0) -> dict:
    key = jax.random.key(seed)
    ks = iter(jax.random.split(key, 64))
    f32 = jnp.float32

    def nrm(shape, scale):
        return scale * jax.random.normal(next(ks), shape, f32)

    def gain(n):
        return 1.0 + nrm((DEPTH, n), 0.02)

    L = DEPTH
    inp = {}
    inp['x_prompt'] = nrm((BATCH, SEQ, D_MODEL), 1.0)
    inp['x_sample'] = nrm((DEC_BATCH, DEC_SEQ, D_MODEL), 1.0)
    inp['state_ssm_re'] = nrm((L, DEC_BATCH, B_GROUPS, B_STATE), 0.1)
    inp['state_ssm_im'] = nrm((L, DEC_BATCH, B_GROUPS, B_STATE), 0.1)
    inp['g_ffn1_pre'] = gain(D_MODEL)
    inp['w_ffn1_gate'] = nrm((L, D_MODEL, D_FF), D_MODEL ** -0.5)
    inp['w_ffn1_up'] = nrm((L, D_MODEL, D_FF), D_MODEL ** -0.5)
    inp['w_ffn1_down'] = nrm((L, D_FF, D_MODEL), D_FF ** -0.5)
    inp['g_ffn1_post'] = gain(D_MODEL)
    inp['g_mix_pre'] = gain(D_MODEL)
    inp['w_in'] = nrm((L, D_MODEL, D_IN), D_MODEL ** -0.5)
    inp['gmlp_g_v'] = gain(A_WIDTH)
    inp['gmlp_w_s'] = nrm((L, A_HEADS, A_CHUNK, A_CHUNK), A_CHUNK ** -0.5)
    inp['gmlp_b_s'] = 1.0 + nrm((L, A_HEADS, A_CHUNK), 0.02)
    inp['s5_lam_re'] = -0.5 + nrm((L, B_GROUPS, B_STATE), 0.01)
    inp['s5_lam_im'] = jnp.pi * jnp.arange(B_STATE, dtype=f32) + nrm((L, B_GROUPS, B_STATE), 0.01)
    inp['s5_log_dt'] = jax.random.uniform(next(ks), (L, B_GROUPS), f32, math.log(DT_MIN), math.log(DT_MAX))
    inp['s5_b_re'] = nrm((L, B_GROUPS, B_STATE, B_GROUP), (2.0 * B_GROUP) ** -0.5)
    inp['s5_b_im'] = nrm((L, B_GROUPS, B_STATE, B_GROUP), (2.0 * B_GROUP) ** -0.5)
    inp['s5_c_re'] = nrm((L, B_GROUPS, B_GROUP, B_STATE), (2.0 * B_STATE) ** -0.5)
    inp['s5_c_im'] = nrm((L, B_GROUPS, B_GROUP, B_STATE), (2.0 * B_STATE) ** -0.5)
    inp['s5_d'] = nrm((L, B_WIDTH), 1.0)
    inp['s5_w_glu'] = nrm((L, B_WIDTH, B_WIDTH), B_WIDTH ** -0.5)
    inp['s5_b_glu'] = nrm((L, B_WIDTH), 0.02)
    inp['g_a_out'] = gain(A_WIDTH)
    inp['g_b_out'] = gain(B_WIDTH)
    inp['w_out'] = nrm((L, D_MIX, D_MODEL), D_MIX ** -0.5)
    inp['g_mix_post'] = gain(D_MODEL)
    inp['g_ffn2_pre'] = gain(D_MODEL)
    inp['w_ffn2_gate'] = nrm((L, D_MODEL, D_FF), D_MODEL ** -0.5)
    inp['w_ffn2_up'] = nrm((L, D_MODEL, D_FF), D_MODEL ** -0.5)
    inp['w_ffn2_down'] = nrm((L, D_FF, D_MODEL), D_FF ** -0.5)
    inp['g_ffn2_post'] = gain(D_MODEL)
    return inp


def reference(x_prompt, x_sample, state_ssm_re, state_ssm_im,
              g_ffn1_pre, w_ffn1_gate, w_ffn1_up, w_ffn1_down, g_ffn1_post,
              g_mix_pre, w_in, gmlp_g_v, gmlp_w_s, gmlp_b_s,
              s5_lam_re, s5_lam_im, s5_log_dt, s5_b_re, s5_b_im, s5_c_re, s5_c_im, s5_d,
              s5_w_glu, s5_b_glu, g_a_out, g_b_out, w_out, g_mix_post,
              g_ffn2_pre, w_ffn2_gate, w_ffn2_up, w_ffn2_down, g_ffn2_post):
    if x_sample.shape[1] > CHUNK:
        raise ValueError('a later-chunk request holds at most CHUNK frames')
    weights = (g_ffn1_pre, w_ffn1_gate, w_ffn1_up, w_ffn1_down, g_ffn1_post,
               g_mix_pre, w_in, gmlp_g_v, gmlp_w_s, gmlp_b_s,
               s5_lam_re, s5_lam_im, s5_log_dt, s5_b_re, s5_b_im, s5_c_re, s5_c_im, s5_d,
               s5_w_glu, s5_b_glu, g_a_out, g_b_out, w_out, g_mix_post,
               g_ffn2_pre, w_ffn2_gate, w_ffn2_up, w_ffn2_down, g_ffn2_post)
    y_p, y_s = x_prompt, x_sample
    fin_p_list, fin_s_list, v_s_list = [], [], []
    for l in range(DEPTH):
        lw = [w[l] for w in weights]
        s0_p = jnp.zeros((x_prompt.shape[0], B_GROUPS, B_STATE), jnp.complex64)
        s0_s = lax.complex(state_ssm_re[l].astype(jnp.float32), state_ssm_im[l].astype(jnp.float32))
        y_p, _, fin_p = _layer(y_p, s0_p, *lw)
        y_s, v_s, fin_s = _layer(y_s, s0_s, *lw)
        fin_p_list.append(fin_p)
        fin_s_list.append(fin_s)
        v_s_list.append(v_s)
    fin_p = jnp.stack(fin_p_list)
    fin_s = jnp.stack(fin_s_list)
    sdt = state_ssm_re.dtype
    state_ssm_re_prompt = fin_p.real.astype(sdt)
    state_ssm_im_prompt = fin_p.imag.astype(sdt)
    state_ssm_re_sample = fin_s.real.astype(sdt)
    state_ssm_im_sample = fin_s.imag.astype(sdt)
    state_gmlp_v_sample = jnp.stack(v_s_list)
    return (y_p, y_s, state_ssm_re_prompt, state_ssm_im_prompt, state_ssm_re_sample, state_ssm_im_sample, state_gmlp_v_sample)
```

```python
import os
from contextlib import ExitStack
import numpy as np
import concourse.bass as bass
import concourse.mybir as mybir
from concourse.bass_utils import run_bass_kernel_spmd

F32 = mybir.dt.float32
BF16 = mybir.dt.bfloat16
AF = mybir.ActivationFunctionType
ALU = mybir.AluOpType

D = 1024
DFF = 2816
NFC = 22
NPIECE = 11
EPS = 1e-6
PI = float(np.pi)


class Emit:
    def __init__(self, nc):
        self.nc = nc
        self.eng = {'pe': nc.tensor, 'act': nc.scalar, 'dve': nc.vector,
                    'pool': nc.gpsimd, 'sp': nc.sync}
        self.sem = {}
        self.cnt = {}
        for e in ['pe', 'act', 'dve', 'pool']:
            self.sem[e] = nc.alloc_semaphore(name=f"s_{e}")
            self.cnt[e] = 0
        self.lastw = {}
        self.readers = {}
        self.waited = {}

    def _sem(self, key):
        if key not in self.sem:
            self.sem[key] = self.nc.alloc_semaphore(name=("s_" + key).replace(':', '_'))
            self.cnt[key] = 0
        return self.sem[key]

    def _wait(self, engine, k, v):
        if v <= 0 or self.waited.get((engine, k), 0) >= v:
            return
        self.eng[engine].wait_ge(self._sem(k), v)
        self.waited[(engine, k)] = v

    def _deps(self, engine, reads, writes):
        toks = {}

        def add(t):
            if t is not None and toks.get(t[0], 0) < t[1]:
                toks[t[0]] = t[1]
        for r in reads:
            add(self.lastw.get(r))
        for w in writes:
            add(self.lastw.get(w))
            for t in self.readers.get(w, ()):
                add(t)
        for k, v in toks.items():
            if k == engine and engine == 'pe':
                continue
            self._wait(engine, k, v)

    def _record(self, tok, reads, writes):
        for r in reads:
            lst = self.readers.setdefault(r, [])
            lst[:] = [t for t in lst if t[0] != tok[0]]
            lst.append(tok)
        for w in writes:
            self.lastw[w] = tok
            self.readers[w] = []

    def op(self, engine, fn, reads=(), writes=()):
        self._deps(engine, reads, writes)
        inst = fn(self.eng[engine])
        self.cnt[engine] += 1
        inst.then_inc(self.sem[engine], 1)
        self._record((engine, self.cnt[engine]), reads, writes)

    def dma(self, queue, slot, out, in_, reads=(), writes=(), **kw):
        if slot in ('setup', 'cast', 'gpost', 'sto'):
            self._rr = getattr(self, '_rr', 0) + 1
            slot = f"{slot}{self._rr % 6}"
        key = 'dma:' + slot
        sem = self._sem(key)
        self._wait(queue, key, self.cnt[key])
        self._deps(queue, reads, writes)
        inst = self.eng[queue].dma_start(out=out, in_=in_, **kw)
        self.cnt[key] += 16
        inst.then_inc(sem, 16)
        self._record((key, self.cnt[key]), reads, writes)

    def barrier(self):
        for e in ['pe', 'act', 'dve', 'pool', 'sp']:
            for k, v in self.cnt.items():
                if k == e and e == 'pe':
                    continue
                self._wait(e, k, v)
        self.lastw.clear()
        self.readers.clear()


def build_nc(L, NBLK, LPRE=0):
    SEG = NBLK * 1024
    NSEG = L // SEG
    NPRE = LPRE // SEG
    NMC = NBLK * 128
    nc = bass.Bass("TRN2", target_bir_lowering=False)
    E = Emit(nc)

    def din(name, shape):
        return nc.dram_tensor(name, list(shape), F32, kind="ExternalInput")

    def dout(name, shape):
        return nc.dram_tensor(name, list(shape), F32, kind="ExternalOutput")

    xp = din("xp", [L, D]); xs = din("xs", [16, D])
    xpre = din("xpre", [max(LPRE, 1), D])
    st_re = din("st_re", [32, 64]); st_im = din("st_im", [32, 64])
    W = {}
    for nm, shp in [("g_ffn1_pre", [D]), ("w_ffn1_gate", [D, DFF]), ("w_ffn1_up", [D, DFF]), ("w_ffn1_down", [DFF, D]),
                    ("g_ffn1_post", [D]), ("g_mix_pre", [D]), ("w_in", [D, 1536]), ("gmlp_g_v", [512]),
                    ("gmlp_w_s", [4, 128, 128]), ("gmlp_b_s", [4, 128]), ("s5_lam_re", [32, 64]), ("s5_lam_im", [32, 64]),
                    ("s5_log_dt", [32]), ("s5_b_re", [32, 64, 16]), ("s5_b_im", [32, 64, 16]), ("s5_c_re", [32, 16, 64]),
                    ("s5_c_im", [32, 16, 64]), ("s5_d", [512]), ("s5_w_glu", [512, 512]), ("s5_b_glu", [512]),
                    ("g_a_out", [512]), ("g_b_out", [512]), ("w_out", [D, D]), ("g_mix_post", [D]),
                    ("g_ffn2_pre", [D]), ("w_ffn2_gate", [D, DFF]), ("w_ffn2_up", [D, DFF]), ("w_ffn2_down", [DFF, D]),
                    ("g_ffn2_post", [D])]:
        W[nm] = din(nm, shp)
    yp = dout("yp", [L, D]); ys = dout("ys", [16, D])
    sp_re = dout("sp_re", [32, 64]); sp_im = dout("sp_im", [32, 64])
    ss_re = dout("ss_re", [32, 64]); ss_im = dout("ss_im", [32, 64])
    vs = dout("vs", [16, 512])

    def dscr(name, shape, dt):
        return nc.dram_tensor(name, list(shape), dt, kind="Internal")
    wg_s = [dscr(f"wg_s{i}", [NPIECE, 128, 8, 256], BF16) for i in range(2)]
    wu_s = [dscr(f"wu_s{i}", [NPIECE, 128, 8, 256], BF16) for i in range(2)]
    h_s = dscr("h_s", [SEG, D], F32)
    h2_s = dscr("h2_s", [SEG, D], F32)
    u_s = dscr("u_s", [max(NBLK, NPRE * NBLK), 128, 32, 128], BF16)
    lam_s = dscr("lam_s", [10, 128, 32, 128], F32)
    nt_s = dscr("nt_s", [NBLK, 128, 8, 1024], BF16)
    wst_s = dscr("wst_s", [128, 32, 128], BF16)
    kt_s = dscr("kt_s", [128, 32, 128], BF16)
    cm_s = dscr("cm_s", [128, 32, 128], BF16)

    top = ExitStack()

    _uid = [0]

    def sb(es, name, shape, dt=F32):
        _uid[0] += 1
        return es.enter_context(nc.sbuf_tensor(f"{name}_{_uid[0]}", list(shape), dt))

    NCD = dict(allow_slow_non_contiguous=True)
    ps = [top.enter_context(nc.psum_tensor(f"ps{i}", [128, 512], F32)) for i in range(6)]
    pT = top.enter_context(nc.psum_tensor("pT", [128, 1024], BF16))
    pTb = top.enter_context(nc.psum_tensor("pTb", [128, 1024], BF16))
    K_DUALT = os.environ.get("K_DUALT", "1") == "1"; K_ROT = os.environ.get("K_ROT", "1") == "1"; K_SPLIT = os.environ.get("K_SPLIT", "1") == "1"
    pTs = [pT, pTb] if K_DUALT else [pT, pT]; pTn = ['pT', 'pTb'] if K_DUALT else ['pT', 'pT']
    PSN = [f"ps{i}" for i in range(6)]

    ident_f = sb(top, "ident_f", [128, 128]); ident_b = sb(top, "ident_b", [128, 128], BF16)
    gc_f1 = sb(top, "gc_f1", [128, 8]); gc_mx = sb(top, "gc_mx", [128, 8]); gc_f2 = sb(top, "gc_f2", [128, 8])
    gc_a = sb(top, "gc_a", [128, 4]); gc_b = sb(top, "gc_b", [128, 4])
    bsT = sb(top, "bsT", [128, 4]); wsT = sb(top, "wsT", [128, 4, 128], BF16)
    CAR = sb(top, "CAR", [128, 32])
    ss = sb(top, "ss", [128, 16]); rs = sb(top, "rs", [128, 16]); rtmp = sb(top, "rtmp", [128, 16])
    junk = sb(top, "junk", [128, D], BF16)

    def rsqrt_cols(n, ncols, scale, src='ss'):
        s_ = ss[0:n, 0:ncols]; r_ = rs[0:n, 0:ncols]; t_ = rtmp[0:n, 0:ncols]
        ri = rs.bitcast(mybir.dt.int32)[0:n, 0:ncols]; si = ss.bitcast(mybir.dt.int32)[0:n, 0:ncols]
        E.op('dve', lambda e: e.tensor_scalar(out=s_, in0=s_, scalar1=scale, scalar2=EPS, op0=ALU.mult, op1=ALU.add),
             reads=['ss'], writes=['ss'])
        E.op('dve', lambda e: e.tensor_scalar(out=ri, in0=si, scalar1=1, scalar2=None, op0=ALU.arith_shift_right),
             reads=['ss'], writes=['rs'])
        E.op('dve', lambda e: e.tensor_scalar(out=ri, in0=ri, scalar1=-1, scalar2=0x5f3759df, op0=ALU.mult, op1=ALU.add),
             reads=['rs'], writes=['rs'])
        for _ in range(3):
            E.op('dve', lambda e: e.scalar_tensor_tensor(out=t_, in0=r_, scalar=-0.5, in1=r_, op0=ALU.mult, op1=ALU.mult),
                 reads=['rs'], writes=['rtmp'])
            E.op('dve', lambda e: e.tensor_tensor(out=t_, in0=t_, in1=s_, op=ALU.mult), reads=['rtmp', 'ss'], writes=['rtmp'])
            E.op('dve', lambda e: e.scalar_tensor_tensor(out=r_, in0=t_, scalar=1.5, in1=r_, op0=ALU.add, op1=ALU.mult),
                 reads=['rtmp', 'rs'], writes=['rs'])

    E.op('pool', lambda e: e.memset(ident_f[:], 1.0), writes=['ident_f'])
    E.op('pool', lambda e: e.affine_select(out=ident_f[:], in_=ident_f[:], pattern=[[-1, 128]], compare_op=ALU.is_equal,
                                           fill=0.0, base=0, channel_multiplier=1), reads=['ident_f'], writes=['ident_f'])
    E.op('dve', lambda e: e.tensor_copy(out=ident_b[:], in_=ident_f[:]), reads=['ident_f'], writes=['ident_b'])
    for t, nm, sc in [(gc_f1, "g_ffn1_pre", 1.0), (gc_mx, "g_mix_pre", 1.0), (gc_f2, "g_ffn2_pre", 1.0),
                      (gc_a, "g_a_out", 1.0), (gc_b, "g_b_out", 0.5)]:
        E.dma('sp', 'setup', t[:], W[nm].ap().rearrange("(k p) -> p k", p=128), writes=[nm], **NCD)
        if sc != 1.0:
            E.op('dve', lambda e: e.tensor_scalar(out=t[:], in0=t[:], scalar1=sc, scalar2=None, op0=ALU.mult),
                 reads=[nm], writes=[nm])
    E.dma('sp', 'setup', bsT[:], W["gmlp_b_s"].ap().rearrange("h t -> t h"), writes=['bsT'], **NCD)

    with ExitStack() as es:
        ws_nat = sb(es, "ws_nat", [128, 4, 128]); wsT_f = sb(es, "wsT_f", [128, 4, 128])
        E.dma('sp', 'setup', ws_nat[:], W["gmlp_w_s"].ap().rearrange("h t s -> t h s"), writes=['ws_nat'])
        E.op('pool', lambda e: e.affine_select(out=ws_nat[:], in_=ws_nat[:], pattern=[[0, 4], [-1, 128]], compare_op=ALU.is_ge,
                                               fill=0.0, base=0, channel_multiplier=1), reads=['ws_nat'], writes=['ws_nat'])
        for h in range(4):
            E.op('pe', lambda e: e.transpose(out=ps[5][:, h * 128:(h + 1) * 128], in_=ws_nat[:, h, :], identity=ident_f[:]),
                 reads=['ws_nat', 'ident_f'], writes=['ps5'])
        E.op('dve', lambda e: e.tensor_copy(out=wsT_f[:], in_=ps[5][:].rearrange("p (h t) -> p h t", h=4)),
             reads=['ps5'], writes=['wsT_f'])
        E.op('dve', lambda e: e.tensor_copy(out=wsT[:], in_=wsT_f[:]), reads=['wsT_f'], writes=['wsT'])

        def t32(name):
            return sb(es, name, [128, 32])
        LR = t32("LR"); LI = t32("LI"); DT = t32("DT"); A_ = t32("A_"); B_ = t32("B_"); MAG = t32("MAG")
        BR = t32("BR"); SINB = t32("SINB"); COSB = t32("COSB"); T1 = t32("T1"); T2 = t32("T2")
        La = t32("La"); Lb = t32("Lb"); CA_ = t32("CA_"); CB_ = t32("CB_"); DEN = t32("DEN"); NA = t32("NA")
        PWA = sb(es, "PWA", [128, 9, 32]); PWB = sb(es, "PWB", [128, 9, 32])
        QA = sb(es, "QA", [128, 10, 32]); QB = sb(es, "QB", [128, 10, 32]); QBs = sb(es, "QBs", [128, 10, 32])
        sgn = sb(es, "sgn", [128, 2])
        Jm = sb(es, "Jm", [128, 128]); J2 = sb(es, "J2", [128, 128])
        maskj = sb(es, "maskj", [128, 8]); DD = sb(es, "DD", [128, 32])

        E.op('pool', lambda e: e.memset(maskj[:], 1.0), writes=['maskj'])
        E.op('pool', lambda e: e.affine_select(out=maskj[:], in_=maskj[:], pattern=[[-16, 8]], compare_op=ALU.is_ge, fill=0.0,
                                               base=0, channel_multiplier=1), reads=['maskj'], writes=['maskj'])
        E.op('pool', lambda e: e.affine_select(out=maskj[:], in_=maskj[:], pattern=[[16, 8]], compare_op=ALU.is_ge, fill=0.0,
                                               base=15, channel_multiplier=-1), reads=['maskj'], writes=['maskj'])
        E.op('pool', lambda e: e.memset(Jm[:], 1.0), writes=['Jm']); E.op('pool', lambda e: e.memset(J2[:], 1.0), writes=['J2'])
        E.op('pool', lambda e: e.affine_select(out=Jm[:], in_=Jm[:], pattern=[[1, 128]], compare_op=ALU.is_equal, fill=0.0,
                                               base=-64, channel_multiplier=-1), reads=['Jm'], writes=['Jm'])
        E.op('pool', lambda e: e.affine_select(out=J2[:], in_=J2[:], pattern=[[-1, 128]], compare_op=ALU.is_equal, fill=0.0,
                                               base=-64, channel_multiplier=1), reads=['J2'], writes=['J2'])
        for i, pre in enumerate(["w_ffn1", "w_ffn2"]):
            for dst, nm in [(wg_s[i], pre + "_gate"), (wu_s[i], pre + "_up")]:
                src = W[nm].ap().rearrange("(k p) (c f) -> c p k f", p=128, f=256)
                for c in range(NPIECE):
                    E.dma('pool', 'cast', dst[c], src[c], writes=[f"{nm}_s"])
        def V(fn, reads, writes):
            E.op('dve', fn, reads=reads, writes=writes)

        def tt(o, a, b, op, rn, wn):
            V(lambda e: e.tensor_tensor(out=o, in0=a, in1=b, op=op), rn, wn)

        for half in range(2):
            E.dma('sp', 'setup', LR[half * 64:(half + 1) * 64, :], W["s5_lam_re"].ap().rearrange("g p -> p g"), writes=['LR'], **NCD)
            E.dma('sp', 'setup', LI[half * 64:(half + 1) * 64, :], W["s5_lam_im"].ap().rearrange("g p -> p g"), writes=['LI'], **NCD)
        E.dma('sp', 'setup', DT[:], W["s5_log_dt"].ap().partition_broadcast(128), writes=['DT'])
        E.op('act', lambda e: e.activation(out=DT[:], in_=DT[:], func=AF.Exp), reads=['DT'], writes=['DT'])
        tt(A_[:], LR[:], DT[:], ALU.mult, ['LR', 'DT'], ['A_'])
        tt(B_[:], LI[:], DT[:], ALU.mult, ['LI', 'DT'], ['B_'])
        E.op('act', lambda e: e.activation(out=MAG[:], in_=A_[:], func=AF.Exp), reads=['A_'], writes=['MAG'])

        def sin_of(dst, src, shift, nm):
            V(lambda e: e.tensor_scalar(out=BR[:], in0=src[:], scalar1=shift, scalar2=None, op0=ALU.add), [nm], ['BR'])
            V(lambda e: e.tensor_copy(out=T2[:], in_=BR[:]), ['BR'], ['T2'])
            for kk in range(5):
                thr = (2 * kk + 1) * PI
                V(lambda e: e.tensor_scalar(out=T1[:], in0=T2[:], scalar1=thr, scalar2=-2 * PI, op0=ALU.is_gt, op1=ALU.mult),
                  ['T2'], ['T1'])
                tt(BR[:], BR[:], T1[:], ALU.add, ['BR', 'T1'], ['BR'])
            V(lambda e: e.tensor_scalar(out=BR[:], in0=BR[:], scalar1=-PI, scalar2=PI, op0=ALU.max, op1=ALU.min), ['BR'], ['BR'])
            E.op('act', lambda e: e.activation(out=dst[:], in_=BR[:], func=AF.Sin), reads=['BR'], writes=[nm + '_sin'])
        sin_of(SINB, B_, 0.0, 'B_')
        E.barrier()
        sin_of(COSB, B_, PI / 2, 'B_')
        E.barrier()
        tt(La[:], MAG[:], COSB[:], ALU.mult, [], ['La'])
        tt(Lb[:], MAG[:], SINB[:], ALU.mult, [], ['Lb'])

        def cmul(za, zb, xa, xb, ya, yb):
            tt(T1[:], xa, ya, ALU.mult, ['cm'], ['cm']); tt(T2[:], xb, yb, ALU.mult, ['cm'], ['cm'])
            tt(za, T1[:], T2[:], ALU.subtract, ['cm'], ['cm'])
            tt(T1[:], xa, yb, ALU.mult, ['cm'], ['cm']); tt(T2[:], xb, ya, ALU.mult, ['cm'], ['cm'])
            tt(zb, T1[:], T2[:], ALU.add, ['cm'], ['cm'])
        V(lambda e: e.memset(PWA[:, 0, :], 1.0), ['cm'], ['cm']); V(lambda e: e.memset(PWB[:, 0, :], 0.0), ['cm'], ['cm'])
        for k in range(1, 9):
            cmul(PWA[:, k, :], PWB[:, k, :], PWA[:, k - 1, :], PWB[:, k - 1, :], La[:], Lb[:])
        V(lambda e: e.tensor_copy(out=QA[:, 0, :], in_=PWA[:, 8, :]), ['cm'], ['cm'])
        V(lambda e: e.tensor_copy(out=QB[:, 0, :], in_=PWB[:, 8, :]), ['cm'], ['cm'])
        for l in range(1, 10):
            cmul(QA[:, l, :], QB[:, l, :], QA[:, l - 1, :], QB[:, l - 1, :], QA[:, l - 1, :], QB[:, l - 1, :])
        V(lambda e: e.memset(sgn[0:64, 0:1], -1.0), ['cm'], ['cm']); V(lambda e: e.memset(sgn[64:128, 0:1], 1.0), ['cm'], ['cm'])
        V(lambda e: e.memset(sgn[0:64, 1:2], 1.0), ['cm'], ['cm']); V(lambda e: e.memset(sgn[64:128, 1:2], -1.0), ['cm'], ['cm'])
        V(lambda e: e.tensor_scalar(out=QBs[:], in0=QB[:], scalar1=sgn[:, 1:2], scalar2=None, op0=ALU.mult), ['cm'], ['cm'])
        V(lambda e: e.tensor_scalar(out=NA[:], in0=La[:], scalar1=-1.0, scalar2=None, op0=ALU.add), ['cm'], ['cm'])
        tt(T1[:], LR[:], LR[:], ALU.mult, ['cm'], ['cm']); tt(T2[:], LI[:], LI[:], ALU.mult, ['cm'], ['cm'])
        tt(DEN[:], T1[:], T2[:], ALU.add, ['cm'], ['cm'])
        V(lambda e: e.reciprocal(out=DEN[:], in_=DEN[:]), ['cm'], ['cm'])
        tt(T1[:], NA[:], LR[:], ALU.mult, ['cm'], ['cm']); tt(T2[:], Lb[:], LI[:], ALU.mult, ['cm'], ['cm'])
        tt(CA_[:], T1[:], T2[:], ALU.add, ['cm'], ['cm']); tt(CA_[:], CA_[:], DEN[:], ALU.mult, ['cm'], ['cm'])
        tt(T1[:], Lb[:], LR[:], ALU.mult, ['cm'], ['cm']); tt(T2[:], NA[:], LI[:], ALU.mult, ['cm'], ['cm'])
        tt(CB_[:], T1[:], T2[:], ALU.subtract, ['cm'], ['cm']); tt(CB_[:], CB_[:], DEN[:], ALU.mult, ['cm'], ['cm'])
        E.barrier()

        def bc(t2d):
            return t2d.unsqueeze(2).to_broadcast([128, 32, 16])
        B1 = sb(es, "B1", [128, 32, 16]); B2 = sb(es, "B2", [128, 32, 16])
        Bst = sb(es, "Bst", [128, 32, 16]); Bsw = sb(es, "Bsw", [128, 32, 16]); TB = sb(es, "TB", [128, 32, 16])
        CAt = sb(es, "CAt", [128, 32, 16]); CBt = sb(es, "CBt", [128, 32, 16])
        bre = W["s5_b_re"].ap().rearrange("g p m -> p g m"); bim = W["s5_b_im"].ap().rearrange("g p m -> p g m")
        cre = W["s5_c_re"].ap().rearrange("g n p -> p g n"); cim = W["s5_c_im"].ap().rearrange("g n p -> p g n")
        E.dma('sp', 'setup', B1[0:64], bre, writes=['B1'], **NCD); E.dma('sp', 'setup', B1[64:128], bim, writes=['B1'], **NCD)
        E.dma('sp', 'setup', B2[0:64], bim, writes=['B2'], **NCD); E.dma('sp', 'setup', B2[64:128], bre, writes=['B2'], **NCD)
        E.dma('sp', 'setup', CAt[0:64], cre, writes=['CAt'], **NCD); E.dma('sp', 'setup', CAt[64:128], cim, writes=['CAt'], **NCD)
        E.dma('sp', 'setup', CBt[0:64], cim, writes=['CBt'], **NCD); E.dma('sp', 'setup', CBt[64:128], cre, writes=['CBt'], **NCD)
        for j in range(8):
            E.dma('sp', 'setup', DD[16 * j:16 * j + 16, :], W["s5_d"].ap().rearrange("(g m) -> m g", m=16), writes=['DD'], **NCD)
        E.barrier()
        V(lambda e: e.tensor_scalar(out=B2[:], in0=B2[:], scalar1=sgn[:, 0:1], scalar2=None, op0=ALU.mult), [], [])
        V(lambda e: e.tensor_scalar(out=CAt[:], in0=CAt[:], scalar1=sgn[:, 1:2], scalar2=None, op0=ALU.mult), [], [])
        V(lambda e: e.tensor_scalar(out=CBt[:], in0=CBt[:], scalar1=-1.0, scalar2=None, op0=ALU.mult), [], [])
        E.barrier()
        tt(Bst[:], B1[:], bc(CA_[:]), ALU.mult, ['cm'], ['cm']); tt(TB[:], B2[:], bc(CB_[:]), ALU.mult, ['cm'], ['cm'])
        tt(Bst[:], Bst[:], TB[:], ALU.add, ['cm'], ['cm'])
        tt(Bsw[:], B2[:], bc(CA_[:]), ALU.mult, ['cm'], ['cm']); tt(TB[:], B1[:], bc(CB_[:]), ALU.mult, ['cm'], ['cm'])
        tt(Bsw[:], Bsw[:], TB[:], ALU.subtract, ['cm'], ['cm'])
        E.barrier()
        W7 = sb(es, "W7", [128, 32, 8, 16]); CP = sb(es, "CP", [128, 32, 8, 16]); CPE = sb(es, "CPE", [128, 32, 8, 16])
        BRep = sb(es, "BRep", [128, 32, 8, 16])
        for j in range(8):
            tt(W7[:, :, j, :], Bst[:], bc(PWA[:, 7 - j, :]), ALU.mult, ['cm'], ['cm'])
            tt(TB[:], Bsw[:], bc(PWB[:, 7 - j, :]), ALU.mult, ['cm'], ['cm'])
            tt(W7[:, :, j, :], W7[:, :, j, :], TB[:], ALU.add, ['cm'], ['cm'])
            tt(CP[:, :, j, :], CAt[:], bc(PWA[:, j + 1, :]), ALU.mult, ['cm'], ['cm'])
            tt(TB[:], CBt[:], bc(PWB[:, j + 1, :]), ALU.mult, ['cm'], ['cm'])
            tt(CP[:, :, j, :], CP[:, :, j, :], TB[:], ALU.add, ['cm'], ['cm'])
            tt(CPE[:, :, j, :], CAt[:], bc(PWA[:, j, :]), ALU.mult, ['cm'], ['cm'])
            tt(TB[:], CBt[:], bc(PWB[:, j, :]), ALU.mult, ['cm'], ['cm'])
            tt(CPE[:, :, j, :], CPE[:, :, j, :], TB[:], ALU.add, ['cm'], ['cm'])
            V(lambda e: e.tensor_copy(out=BRep[:, :, j, :], in_=Bst[:]), ['cm'], ['cm'])
        E.barrier()
        tt(Jm[:], Jm[:], J2[:], ALU.add, ['cm'], ['cm'])
        E.barrier()
        stg = sb(es, "stg", [128, 32, 128], BF16)
        for g in range(32):
            b_ = ps[g % 4]
            E.op('pe', lambda e: e.transpose(out=b_[:, 0:128], in_=W7[:, g, :, :].rearrange("p j m -> p (j m)"), identity=ident_f[:]),
                 reads=['ident_f'], writes=[PSN[g % 4]])
            E.op('act', lambda e: e.copy(out=stg[:, g, :], in_=b_[:, 0:128]), reads=[PSN[g % 4]], writes=['stg'])
        E.dma('sp', 'setup', wst_s[:], stg[:], reads=['stg'], writes=['wst_s'])
        V(lambda e: e.tensor_copy(out=stg[:], in_=CP[:].rearrange("p g t n -> p g (t n)")), ['stg'], ['stg'])
        E.dma('sp', 'setup', cm_s[:], stg[:], reads=['stg'], writes=['cm_s'])
        KT = sb(es, "KT", [128, 8, 16])
        idv = ident_f[:].rearrange("p (t n) -> p t n", t=8)
        for g in range(32):
            b_ = ps[g % 4]
            E.op('pe', lambda e: e.matmul(b_[:, 0:128], lhsT=BRep[:, g, :, :].rearrange("p j m -> p (j m)"),
                                          rhs=CPE[:, g, :, :].rearrange("p e n -> p (e n)"), start=True, stop=True),
                 reads=[], writes=[PSN[g % 4]])
            Gv = b_[:, 0:128].rearrange("p (e n) -> p e n", e=8)
            V(lambda e: e.tensor_scalar(out=KT[:], in0=idv, scalar1=DD[:, g:g + 1], scalar2=None, op0=ALU.mult), ['KT'], ['KT'])
            for j in range(8):
                V(lambda e: e.scalar_tensor_tensor(out=KT[:, j:8, :], in0=Gv[:, 0:8 - j, :], scalar=maskj[:, j:j + 1],
                                                   in1=KT[:, j:8, :], op0=ALU.mult, op1=ALU.add), ['KT', PSN[g % 4]], ['KT'])
            V(lambda e: e.tensor_copy(out=stg[:, g, :], in_=KT[:].rearrange("p t n -> p (t n)")), ['KT', 'stg'], ['KT', 'stg'])
        E.dma('sp', 'setup', kt_s[:], stg[:], reads=['stg'], writes=['kt_s'])
        LAMt = [sb(es, f"LAMt{i}", [128, 32, 128]) for i in range(2)]
        for l in range(10):
            Lt = LAMt[l % 2]; nm = f"LAMt{l % 2}"
            for g in range(32):
                rn = f"{nm}g{g}"
                E.op('act', lambda e: e.mul(out=Lt[:, g, :], in_=ident_f[:], mul=QA[:, l, g:g + 1]),
                     reads=[], writes=[rn])
            for g in range(32):
                rn = f"{nm}g{g}"
                E.op('dve', lambda e: e.scalar_tensor_tensor(out=Lt[:, g, :], in0=Jm[:], scalar=QBs[:, l, g:g + 1], in1=Lt[:, g, :],
                                                             op0=ALU.mult, op1=ALU.add), reads=[rn], writes=[rn])
            E.dma('sp', 'setup', lam_s[l], Lt[:], reads=[f"{nm}g{g}" for g in range(32)], writes=['lam_s'])
        E.barrier()

    class Site:
        pass

    def mksite(es, name, ncols):
        st_ = Site()
        st_.ss = sb(es, name + "_ss", [128, ncols]); st_.rs = sb(es, name + "_rs", [128, ncols]); st_.rt = sb(es, name + "_rt", [128, ncols])
        st_.n = name
        return st_

    def rsqrt_site(site, n, c0, c1, scale, iters=2):
        nm = site.n
        s_ = site.ss[0:n, c0:c1]; r_ = site.rs[0:n, c0:c1]; t_ = site.rt[0:n, c0:c1]
        ri = site.rs.bitcast(mybir.dt.int32)[0:n, c0:c1]; si = site.ss.bitcast(mybir.dt.int32)[0:n, c0:c1]
        E.op('dve', lambda e: e.tensor_scalar(out=s_, in0=s_, scalar1=scale, scalar2=EPS, op0=ALU.mult, op1=ALU.add),
             reads=[nm + 'ss'], writes=[nm + 'ss'])
        E.op('dve', lambda e: e.tensor_scalar(out=ri, in0=si, scalar1=1, scalar2=None, op0=ALU.arith_shift_right),
             reads=[nm + 'ss'], writes=[nm + 'rs'])
        E.op('dve', lambda e: e.tensor_scalar(out=ri, in0=ri, scalar1=-1, scalar2=0x5f3759df, op0=ALU.mult, op1=ALU.add),
             reads=[nm + 'rs'], writes=[nm + 'rs'])
        for _ in range(iters):
            E.op('dve', lambda e: e.scalar_tensor_tensor(out=t_, in0=r_, scalar=-0.5, in1=r_, op0=ALU.mult, op1=ALU.mult),
                 reads=[nm + 'rs'], writes=[nm + 'rt'])
            E.op('dve', lambda e: e.tensor_tensor(out=t_, in0=t_, in1=s_, op=ALU.mult), reads=[nm + 'rt', nm + 'ss'], writes=[nm + 'rt'])
            E.op('dve', lambda e: e.scalar_tensor_tensor(out=r_, in0=t_, scalar=1.5, in1=r_, op0=ALU.add, op1=ALU.mult),
                 reads=[nm + 'rt', nm + 'rs'], writes=[nm + 'rs'])

    def transpose8(srcb, srcname, nt_, dstT, dstname, col0, gcol):
        for hf in range(2):
            pb_ = pTs[hf]
            for kk in range(4):
                k = hf * 4 + kk
                E.op('pe', lambda e: e.transpose(out=pb_[:, kk * 128:kk * 128 + nt_],
                                                 in_=srcb[0:nt_, k * 128:(k + 1) * 128], identity=ident_b[0:nt_, 0:nt_]),
                     reads=[srcname, 'ident_b'], writes=[pTn[hf]])
            E.op('dve', lambda e: e.tensor_tensor(out=dstT[:, hf * 4:(hf + 1) * 4, col0:col0 + nt_],
                                                  in0=pb_[:, 0:512].rearrange("p (k t) -> p k t", k=4)[:, :, 0:nt_],
                                                  in1=gcol[:, hf * 4:(hf + 1) * 4].unsqueeze(2).to_broadcast([128, 4, nt_]), op=ALU.mult),
                 reads=[pTn[hf]], writes=[dstname])

    def ffn_phase(src, dst, ntok, nt, wi, wd_name, gcol, gpost_name, xu=None, save_nT=False):
        with ExitStack() as es:
            wd = sb(es, "wd", [128, NFC, D], BF16)
            gpost_t = sb(es, "gpost_t", [128, D])
            E.dma('sp', 'gpost', gpost_t[:], W[gpost_name].ap().partition_broadcast(128), writes=['gpost_t'])
            E.op('dve', lambda e: e.tensor_scalar(out=gpost_t[:], in0=gpost_t[:], scalar1=0.5, scalar2=None, op0=ALU.mult),
                 reads=['gpost_t'], writes=['gpost_t'])
            wgb = [sb(es, f"wgb{i}", [128, 8, 256], BF16) for i in range(2)]
            wub = [sb(es, f"wub{i}", [128, 8, 256], BF16) for i in range(2)]
            xsl = [sb(es, f"xsl{i}", [128, D]) for i in range(8)]
            xn = [sb(es, f"xn{i}", [128, D], BF16) for i in range(4)]
            xnT = [sb(es, f"xnT{i}", [128, 8, 512], BF16) for i in range(2)]
            hid = sb(es, "hid", [128, NFC, 512], BF16)
            sg = [sb(es, f"sg{i}", [128, 512]) for i in range(2)]
            ot = [sb(es, f"ot{i}", [128, D]) for i in range(2)]
            sH = mksite(es, "sH", 4); sP = [mksite(es, f"sP{i}", 2) for i in range(2)]
            if xu is not None:
                ncb = xu
                ntokb = ncb * 8
                sX = mksite(es, "sX", 2)
                hn = [sb(es, f"hnx{i}", [128, D], BF16) for i in range(2)]
                nT = sb(es, "nTx", [128, 8, 1024], BF16)
                X4 = sb(es, "X4x", [128, 32, 8, 16], BF16)
                Ub = sb(es, "Ubx", [128, 32, 128], BF16)
                winb = sb(es, "winbx", [128, 8, 512], BF16)
                E.dma('pool', 'winb', winb[:], W["w_in"].ap().rearrange("(k p) n -> p k n", p=128)[:, :, 1024:1536], writes=['winb'])
            E.dma('pool', 'wd', wd[:], W[wd_name].ap().rearrange("(c p) d -> p c d", p=128), writes=['wd'])
            nsubs_total = ntok // nt
            macros = [(m0, min(4, nsubs_total - m0)) for m0 in range(0, nsubs_total, 4)]
            pend = []

            dnc = [0]

            def flush(upto=None):
                fl = [it for it in pend if upto is None or it[0] <= upto]
                rest = [it for it in pend if not (upto is None or it[0] <= upto)]
                pend.clear(); pend.extend(rest)
                for _, f in fl:
                    f()

            def head_load(mi):
                m0, nsub = macros[mi]
                for st in range(nsub):
                    sl = (m0 + st) % 8
                    tok = (m0 + st) * nt
                    E.dma('sp', f'x{sl}', xsl[sl][0:nt, :], src[tok:tok + nt, :], writes=[f'xsl{sl}'])

            def head_sq(mi, sts):
                m0, nsub = macros[mi]
                for st in sts:
                    if st >= nsub:
                        continue
                    sl = (m0 + st) % 8
                    E.op('act', lambda e: e.activation(out=junk[0:nt, :], in_=xsl[sl][0:nt, :], func=AF.Square,
                                                       accum_out=sH.ss[0:nt, st:st + 1]), reads=[f'xsl{sl}'], writes=['junk', 'sHss'])

            def head_rs(mi):
                m0, nsub = macros[mi]
                rsqrt_site(sH, nt, 0, nsub, 1.0 / D)

            def head_mul(mi, sts):
                m0, nsub = macros[mi]
                for st in sts:
                    if st >= nsub:
                        continue
                    sl = (m0 + st) % 8
                    xb2 = xn[st]; xn2 = f'xn{st}'
                    E.op('act', lambda e: e.mul(out=xb2[0:nt, :], in_=xsl[sl][0:nt, :], mul=sH.rs[0:nt, st:st + 1]),
                         reads=[f'xsl{sl}', 'sHrs'], writes=[xn2])

            def head_pre(mi):
                head_sq(mi, range(4)); head_rs(mi); head_mul(mi, range(4))

            def head_T(mi):
                m0, nsub = macros[mi]
                xb_ = xnT[mi % 2]; xbn = f'xnT{mi % 2}'
                for st in range(nsub):
                    transpose8(xn[st], f'xn{st}', nt, xb_, xbn, st * nt, gcol)

            piece_ctr = [0]

            def gate_up(mi, hooks=None):
                m0, nsub = macros[mi]
                NT = nsub * nt
                xb_ = xnT[mi % 2]; xbn = f'xnT{mi % 2}'
                for pc in range(NPIECE):
                    bsl = piece_ctr[0] % 2; piece_ctr[0] += 1
                    E.dma('sp', f'wg{bsl}', wgb[bsl][:], wg_s[wi][pc], reads=[f"w_ffn{wi + 1}_gate_s"], writes=[f'wgb{bsl}'])
                    E.dma('sp', f'wu{bsl}', wub[bsl][:], wu_s[wi][pc], reads=[f"w_ffn{wi + 1}_up_s"], writes=[f'wub{bsl}'])
                    for f2 in range(2):
                        fc = pc * 2 + f2
                        pg = ps[fc % 2]; pu = ps[2]; ng = PSN[fc % 2]; nu = PSN[2]
                        for k in range(8):
                            E.op('pe', lambda e: e.matmul(pg[:, 0:NT], lhsT=wgb[bsl][:, k, f2 * 128:(f2 + 1) * 128],
                                                          rhs=xb_[:, k, 0:NT], start=(k == 0), stop=(k == 7)),
                                 reads=[f'wgb{bsl}', xbn], writes=[ng])
                        for k in range(8):
                            E.op('pe', lambda e: e.matmul(pu[:, 0:NT], lhsT=wub[bsl][:, k, f2 * 128:(f2 + 1) * 128],
                                                          rhs=xb_[:, k, 0:NT], start=(k == 0), stop=(k == 7)),
                                 reads=[f'wub{bsl}', xbn], writes=[nu])
                        sgb = sg[fc % 2]; sgn_ = f'sg{fc % 2}'
                        E.op('act', lambda e: e.activation(out=sgb[:, 0:NT], in_=pg[:, 0:NT], func=AF.Silu), reads=[ng], writes=[sgn_])
                        E.op('dve', lambda e: e.tensor_tensor(out=hid[:, fc, 0:NT], in0=sgb[:, 0:NT], in1=pu[:, 0:NT], op=ALU.mult),
                             reads=[sgn_, nu], writes=['hid'])
                    if hooks and pc in hooks:
                        hooks[pc]()

            octr = [0]

            def xu_T(o_, on, bpos):
                def f():
                    b = (bpos // nt) % 2
                    E.op('act', lambda e: e.activation(out=junk[0:nt, :], in_=o_[0:nt, :], func=AF.Square,
                                                       accum_out=sX.ss[0:nt, b:b + 1]), reads=[on], writes=['junk', 'sXss'])
                    rsqrt_site(sX, nt, b, b + 1, 1.0 / D)
                    E.op('act', lambda e: e.mul(out=hn[b][0:nt, :], in_=o_[0:nt, :], mul=sX.rs[0:nt, b:b + 1]),
                         reads=[on, 'sXrs'], writes=[f'hnx{b}'])
                    transpose8(hn[b], f'hnx{b}', nt, nT, 'nTx', bpos, gc_mx)
                return f

            def xu_block(blk):
                def f():
                    for j in range(8):
                        b_ = ps[j % 2]
                        for k in range(8):
                            E.op('pe', lambda e: e.matmul(b_[0:ncb, :], lhsT=nT[:, k, j:ntokb:8], rhs=winb[:, k, :],
                                                          start=(k == 0), stop=(k == 7)), reads=['nTx', 'winb'], writes=[PSN[j % 2]])
                        E.op('act', lambda e: e.copy(out=X4[0:ncb, :, j, :], in_=b_[0:ncb, :].rearrange("c (g m) -> c g m", g=32)),
                             reads=[PSN[j % 2]], writes=['X4'])
                    if save_nT:
                        E.dma('pool', 'nts', nt_s[blk][:, :, 0:ntokb], nT[:, :, 0:ntokb], reads=['nTx'], writes=['nt_s'])
                    for g4 in range(8):
                        hf = g4 % 2
                        pb_ = pTs[hf]
                        for gg in range(4):
                            g = g4 * 4 + gg
                            E.op('pe', lambda e: e.transpose(out=pb_[:, gg * 128:gg * 128 + ncb],
                                                             in_=X4[0:ncb, g, :, :].rearrange("c j m -> c (j m)"),
                                                             identity=ident_b[0:ncb, 0:ncb]), reads=['X4', 'ident_b'], writes=[pTn[hf]])
                        E.op('dve', lambda e: e.tensor_copy(out=Ub[:, g4 * 4:(g4 + 1) * 4, 0:ncb],
                                                            in_=pb_[:, 0:512].rearrange("p (g c) -> p g c", g=4)[:, :, 0:ncb]),
                             reads=[pTn[hf]], writes=['Ub'])
                    E.dma('pool', 'ub', u_s[blk][:, :, 0:ncb], Ub[:, :, 0:ncb], reads=['Ub'], writes=['u_s'])
                return f

            def down(mi, st):
                m0, nsub = macros[mi]
                sl = (m0 + st) % 8
                dsl = octr[0] % 2
                b0 = (2 * octr[0]) % 3 if K_ROT else 0; b1 = (2 * octr[0] + 1) % 3 if K_ROT else 1
                pd = [ps[3 + b0], ps[3 + b1]]; pdn = [PSN[3 + b0], PSN[3 + b1]]
                sp_ = sP[dsl]
                xb_ = None
                for half in range(2):
                    for fc in range(NFC):
                        E.op('pe', lambda e: e.matmul(pd[half][0:nt, :], lhsT=hid[:, fc, st * nt:(st + 1) * nt],
                                                      rhs=wd[:, fc, half * 512:(half + 1) * 512], start=(fc == 0), stop=(fc == NFC - 1)),
                             reads=['hid', 'wd'], writes=[pdn[half]])
                dnc[0] += 1
                flush(dnc[0] - 2)
                for half in range(2):
                    E.op('act', lambda e: e.activation(out=junk[0:nt, 0:512], in_=pd[half][0:nt, :], func=AF.Square,
                                                       accum_out=sp_.ss[0:nt, half:half + 1]), reads=[pdn[half]], writes=['junk', sp_.n + 'ss'])
                E.op('dve', lambda e: e.tensor_tensor(out=sp_.ss[0:nt, 0:1], in0=sp_.ss[0:nt, 0:1], in1=sp_.ss[0:nt, 1:2], op=ALU.add),
                     reads=[sp_.n + 'ss'], writes=[sp_.n + 'ss'])
                rsqrt_site(sp_, nt, 0, 1, 1.0 / D)
                o_ = ot[octr[0] % 2]; on = f'ot{octr[0] % 2}'; octr[0] += 1
                for half in range(2):
                    E.op('dve', lambda e: e.scalar_tensor_tensor(out=o_[0:nt, half * 512:(half + 1) * 512], in0=pd[half][0:nt, :],
                                                                 scalar=sp_.rs[0:nt, 0:1], in1=gpost_t[0:nt, half * 512:(half + 1) * 512],
                                                                 op0=ALU.mult, op1=ALU.mult),
                         reads=[pdn[half], sp_.n + 'rs', 'gpost_t'], writes=[on])
                E.op('pool', lambda e: e.tensor_tensor(out=o_[0:nt, :], in0=o_[0:nt, :], in1=xsl[sl][0:nt, :], op=ALU.add),
                     reads=[on, f'xsl{sl}'], writes=[on])
                tok = (m0 + st) * nt
                if dst is not None:
                    E.dma('pool', on, dst[tok:tok + nt, :], o_[0:nt, :], reads=[on], writes=['dst'])
                if xu is not None:
                    bpos = tok % ntokb
                    pend.append((dnc[0], xu_T(o_, on, bpos)))
                    if (tok + nt) % ntokb == 0:
                        pend.append((dnc[0], xu_block(tok // ntokb)))

            head_load(0); head_pre(0); head_T(0)
            for mi in range(len(macros)):
                m0, nsub = macros[mi]
                nxt = mi + 1 < len(macros)
                hooks = {}
                if nxt:
                    hooks = {2: (lambda mi=mi: head_load(mi + 1)),
                             4: (lambda mi=mi: head_sq(mi + 1, (0, 1))), 5: (lambda mi=mi: head_sq(mi + 1, (2, 3))),
                             6: (lambda mi=mi: head_rs(mi + 1)),
                             8: (lambda mi=mi: head_mul(mi + 1, (0, 1))), 9: (lambda mi=mi: head_mul(mi + 1, (2, 3)))}
                gate_up(mi, hooks)
                for st in range(nsub):
                    if nxt and nsub >= 2 and st == 1:
                        pend.append((-1, lambda mi=mi: head_T(mi + 1)))
                    down(mi, st)
                    if nxt and nsub < 2:
                        head_T(mi + 1)
            flush()
            E.barrier()

    def load_norm_T(es_bufs, src, tok0, nt, nst, gcol, nT):
        hsl, hn = es_bufs
        for st in range(nst):
            E.dma('sp', f'h{st % 2}', hsl[st % 2][0:nt, :], src[tok0 + st * nt:tok0 + (st + 1) * nt, :], writes=[f'hsl{st % 2}'])
            E.op('act', lambda e: e.activation(out=junk[0:nt, :], in_=hsl[st % 2][0:nt, :], func=AF.Square,
                                               accum_out=ss[0:nt, 0:1]), reads=[f'hsl{st % 2}'], writes=['junk', 'ss'])
            rsqrt_cols(nt, 1, 1.0 / D)
            E.op('dve', lambda e: e.tensor_scalar(out=hn[0:nt, :], in0=hsl[st % 2][0:nt, :], scalar1=rs[0:nt, 0:1],
                                                  scalar2=None, op0=ALU.mult), reads=[f'hsl{st % 2}', 'rs'], writes=['hn'])
            for k in range(8):
                E.op('pe', lambda e: e.transpose(out=pT[:, k * 128:k * 128 + nt], in_=hn[0:nt, k * 128:(k + 1) * 128],
                                                 identity=ident_b[0:nt, 0:nt]), reads=['hn', 'ident_b'], writes=['pT'])
            E.op('dve', lambda e: e.tensor_tensor(out=nT[:, :, st * nt:(st + 1) * nt],
                                                  in0=pT[:].rearrange("p (k t) -> p k t", k=8)[:, :, 0:nt],
                                                  in1=gcol[:].unsqueeze(2).to_broadcast([128, 8, nt]), op=ALU.mult),
                 reads=['pT'], writes=['nT'])

    def xu_phase(src, nblk, ncb, nt):
        nst = (ncb * 8) // nt
        with ExitStack() as es:
            hsl = [sb(es, f"hsl{i}", [128, D]) for i in range(2)]
            hn = sb(es, "hn", [128, D], BF16)
            nT = sb(es, "nT", [128, 8, 1024], BF16)
            X4 = sb(es, "X4", [128, 32, 8, 16], BF16)
            Ub = sb(es, "Ub", [128, 32, 128], BF16)
            winb = sb(es, "winb", [128, 8, 512], BF16)
            E.dma('pool', 'winb', winb[:], W["w_in"].ap().rearrange("(k p) n -> p k n", p=128)[:, :, 1024:1536], writes=['winb'])
            for blk in range(nblk):
                ntokb = ncb * 8
                load_norm_T((hsl, hn), src, blk * ntokb, nt, nst, gc_mx, nT)
                for j in range(8):
                    b_ = ps[j % 2]
                    for k in range(8):
                        E.op('pe', lambda e: e.matmul(b_[0:ncb, :], lhsT=nT[:, k, j:ntokb:8], rhs=winb[:, k, :],
                                                      start=(k == 0), stop=(k == 7)), reads=['nT', 'winb'], writes=[PSN[j % 2]])
                    E.op('act', lambda e: e.copy(out=X4[0:ncb, :, j, :], in_=b_[0:ncb, :].rearrange("c (g m) -> c g m", g=32)),
                         reads=[PSN[j % 2]], writes=['X4'])
                for g8 in range(4):
                    for gg in range(8):
                        g = g8 * 8 + gg
                        E.op('pe', lambda e: e.transpose(out=pT[:, gg * 128:gg * 128 + ncb],
                                                         in_=X4[0:ncb, g, :, :].rearrange("c j m -> c (j m)"),
                                                         identity=ident_b[0:ncb, 0:ncb]), reads=['X4', 'ident_b'], writes=['pT'])
                    E.op('dve', lambda e: e.tensor_copy(out=Ub[:, g8 * 8:(g8 + 1) * 8, 0:ncb],
                                                        in_=pT[:].rearrange("p (g c) -> p g c", g=8)[:, :, 0:ncb]),
                         reads=['pT'], writes=['Ub'])
                E.dma('pool', 'ub', u_s[blk][:, :, 0:ncb], Ub[:, :, 0:ncb], reads=['Ub'], writes=['u_s'])
            E.barrier()

    def scan_phase(nblk, ncb, Pb, lite=False, blk0=0):
        nmc = nblk * ncb
        with ExitStack() as es:
            S = sb(es, "S", [128, 32, nmc + 1])
            Ub = [sb(es, f"Ubs{i}", [128, 32, 128], BF16) for i in range(2)]
            Wst = sb(es, "Wst", [128, 32, 128], BF16)
            LAM = [sb(es, f"LAM{i}", [128, 32, 128]) for i in range(2)]
            E.dma('sp', 'wst', Wst[:], wst_s[:], reads=['wst_s'], writes=['Wst'])
            E.op('dve', lambda e: e.tensor_copy(out=S[:, :, 0], in_=CAR[:]), reads=['CAR'], writes=[f'S{g}' for g in range(32)])
            for blk in range(nblk):
                ub = Ub[blk % 2]; un = f'Ubs{blk % 2}'
                E.dma('sp', un, ub[:, :, 0:ncb], u_s[blk0 + blk][:, :, 0:ncb], reads=['u_s'], writes=[un])
                for g4 in range(8):
                    b_ = ps[g4 % 4]
                    for gg in range(4):
                        g = g4 * 4 + gg
                        E.op('pe', lambda e: e.matmul(b_[:, gg * 128:gg * 128 + ncb], lhsT=Wst[:, g, :], rhs=ub[:, g, 0:ncb],
                                                      start=True, stop=True), reads=['Wst', un], writes=[PSN[g4 % 4]])
                    E.op('act', lambda e: e.copy(out=S[:, g4 * 4:(g4 + 1) * 4, 1 + blk * ncb:1 + (blk + 1) * ncb],
                                                 in_=b_[:].rearrange("p (g c) -> p g c", g=4)[:, :, 0:ncb]),
                         reads=[PSN[g4 % 4]], writes=[f'S{g4 * 4 + i}' for i in range(4)])
            levels = []
            d = 1
            while d < nmc:
                levels.append((d, 2 * d)); d *= 2
            d = nmc
            while d >= 1:
                levels.append((d, d)); d //= 2
                if lite:
                    break
            for li, (d, t0) in enumerate(levels):
                l = int(np.log2(d))
                cnt = len(range(t0, nmc + 1, 2 * d))
                lm = LAM[li % 2]; ln = f'LAM{li % 2}'
                E.dma('sp', ln, lm[:], lam_s[l], reads=['lam_s'], writes=[ln])
                for g in range(32):
                    b_ = ps[g % 6]
                    tgt = S[:, g, t0:nmc + 1:2 * d]
                    srcv = S[:, g, t0 - d:nmc + 1 - d:2 * d]
                    E.op('pe', lambda e: e.matmul(b_[:, 0:cnt], lhsT=lm[:, g, :], rhs=srcv, start=True, stop=True),
                         reads=[ln, f'S{g}'], writes=[PSN[g % 6]])
                    E.op('dve', lambda e: e.tensor_tensor(out=tgt, in0=b_[:, 0:cnt], in1=tgt, op=ALU.add),
                         reads=[PSN[g % 6], f'S{g}'], writes=[f'S{g}'])
            allS = [f'S{g}' for g in range(32)]
            if not lite:
                E.op('dve', lambda e: e.tensor_copy(out=Pb[:, :, 0:nmc], in_=S[:, :, 0:nmc]), reads=allS, writes=['Pb'])
            E.op('dve', lambda e: e.tensor_copy(out=CAR[:], in_=S[:, :, nmc]), reads=allS, writes=['CAR'])
            E.barrier()

    def lite_scan_merged(nseg, nblk, ncb):
        nmc = nblk * ncb
        GC = 8
        with ExitStack() as es:
            S = [sb(es, f"Sm{i}", [128, GC, nseg, nmc + 1]) for i in range(2)]
            Ub = [sb(es, f"Ubl{i}", [128, GC, 128], BF16) for i in range(2)]
            Wst = sb(es, "Wstl", [128, 32, 128], BF16)
            LAM = [sb(es, f"LAMl{i}", [128, GC, 128]) for i in range(2)]
            E.dma('sp', 'wst', Wst[:], wst_s[:], reads=['wst_s'], writes=['Wstl'])
            uctr = 0; lctr = 0
            for gc in range(32 // GC):
                g0 = gc * GC
                Sc = S[gc % 2]; sn = f'Sm{gc % 2}'
                SN = [f'{sn}g{i}' for i in range(GC)]
                for b in range(nseg * nblk):
                    seg, blk = b // nblk, b % nblk
                    ub = Ub[uctr % 2]; un = f'Ubl{uctr % 2}'; uctr += 1
                    E.dma('sp', un, ub[:, :, 0:ncb], u_s[b][:, g0:g0 + GC, 0:ncb], reads=['u_s'], writes=[un])
                    for q in range(GC // 4):
                        b_ = ps[(b * 2 + q) % 6]; bn = PSN[(b * 2 + q) % 6]
                        for gg in range(4):
                            gi = q * 4 + gg
                            E.op('pe', lambda e: e.matmul(b_[:, gg * 128:gg * 128 + ncb], lhsT=Wst[:, g0 + gi, :], rhs=ub[:, gi, 0:ncb],
                                                          start=True, stop=True), reads=['Wstl', un], writes=[bn])
                        E.op('act', lambda e: e.copy(out=Sc[:, q * 4:(q + 1) * 4, seg, 1 + blk * ncb:1 + (blk + 1) * ncb],
                                                     in_=b_[:].rearrange("p (g c) -> p g c", g=4)[:, :, 0:ncb]),
                             reads=[bn], writes=SN[q * 4:(q + 1) * 4])
                d = 1
                while d < nmc:
                    l = int(np.log2(d)); t0 = 2 * d
                    cnt = len(range(t0, nmc + 1, 2 * d))
                    lm = LAM[lctr % 2]; ln = f'LAMl{lctr % 2}'; lctr += 1
                    E.dma('sp', ln, lm[:], lam_s[l][:, g0:g0 + GC, :], reads=['lam_s'], writes=[ln])
                    for gi in range(GC):
                        if nseg * cnt <= 512:
                            parts = [(0, nseg)]
                        else:
                            parts = [(sg_, sg_ + 1) for sg_ in range(nseg)]
                        for pi, (s0, s1) in enumerate(parts):
                            ns_ = s1 - s0
                            bi = (gi * 3 + pi) % 6
                            b_ = ps[bi]; bn = PSN[bi]
                            tgt = Sc[:, gi, s0:s1, t0:nmc + 1:2 * d]
                            srcv = Sc[:, gi, s0:s1, t0 - d:nmc + 1 - d:2 * d]
                            E.op('pe', lambda e: e.matmul(b_[:, 0:ns_ * cnt], lhsT=lm[:, gi, :], rhs=srcv, start=True, stop=True),
                                 reads=[ln, SN[gi]], writes=[bn])
                            E.op('dve', lambda e: e.tensor_tensor(out=tgt, in0=b_[:, 0:ns_ * cnt].rearrange("p (s c) -> p s c", s=ns_),
                                                                  in1=tgt, op=ALU.add), reads=[bn, SN[gi]], writes=[SN[gi]])
                    d *= 2
                l = int(np.log2(nmc))
                lm = LAM[lctr % 2]; ln = f'LAMl{lctr % 2}'; lctr += 1
                E.dma('sp', ln, lm[:], lam_s[l][:, g0:g0 + GC, :], reads=['lam_s'], writes=[ln])
                E.op('dve', lambda e: e.tensor_copy(out=Sc[:, :, 0, 0], in_=CAR[:, g0:g0 + GC]), reads=['CAR'], writes=SN)
                for seg in range(nseg):
                    for gi in range(GC):
                        bi = gi % 6
                        b_ = ps[bi]; bn = PSN[bi]
                        E.op('pe', lambda e: e.matmul(b_[:, 0:1], lhsT=lm[:, gi, :], rhs=Sc[:, gi, seg, 0:1], start=True, stop=True),
                             reads=[ln, SN[gi]], writes=[bn])
                        if seg + 1 < nseg:
                            dstc = Sc[:, gi, seg + 1, 0:1]
                            E.op('dve', lambda e: e.tensor_tensor(out=dstc, in0=b_[:, 0:1], in1=Sc[:, gi, seg, nmc:nmc + 1], op=ALU.add),
                                 reads=[bn, SN[gi]], writes=[SN[gi]])
                        else:
                            E.op('dve', lambda e: e.tensor_tensor(out=CAR[:, g0 + gi:g0 + gi + 1], in0=b_[:, 0:1],
                                                                  in1=Sc[:, gi, seg, nmc:nmc + 1], op=ALU.add),
                                 reads=[bn, SN[gi]], writes=['CAR'])
            E.barrier()

    def mixer_phase(src, dst, nblk, ncb, nt, Pb, vs_out=None):
        nst = (ncb * 8) // nt
        ntokb = ncb * 8
        with ExitStack() as es:
            hsl = [sb(es, f"hsl{i}", [128, D]) for i in range(2)]
            hn = [sb(es, f"hn{i}", [128, D], BF16) for i in range(2)]
            nT = sb(es, "nT", [128, 8, 1024], BF16)
            Ub = sb(es, "Ubm", [128, 32, 128], BF16)
            Kt = sb(es, "Kt", [128, 32, 128], BF16); Cm = sb(es, "Cm", [128, 32, 128], BF16)
            wina = sb(es, "wina", [128, 8, 1024], BF16); wout = sb(es, "wout", [128, 8, D], BF16)
            wglu = sb(es, "wglu", [128, 4, 512], BF16)
            mixT = sb(es, "mixT", [128, 8, 1024], BF16)
            Y = sb(es, "Y", [128, 8, 512])
            ybf = [sb(es, f"ybf{i}", [128, 512], BF16) for i in range(2)]
            y2T = [sb(es, f"y2T{i}", [128, 4, 128], BF16) for i in range(2)]
            gt = [sb(es, f"gt{i}", [128, 512]) for i in range(2)]
            uu = [sb(es, f"uu{i}", [128, 512]) for i in range(2)]; vv = [sb(es, f"vv{i}", [128, 512]) for i in range(2)]
            vnf = [sb(es, f"vnf{i}", [128, 512]) for i in range(2)]; vnb = [sb(es, f"vnb{i}", [128, 512], BF16) for i in range(2)]
            aa = [sb(es, f"aa{i}", [128, 512]) for i in range(2)]; anb = [sb(es, f"anb{i}", [128, 512], BF16) for i in range(2)]
            ot = [sb(es, f"otm{i}", [128, D]) for i in range(2)]
            sL = [mksite(es, f"sL{i}", 1) for i in range(2)]; sB = mksite(es, "sB", 8)
            sV = [mksite(es, f"sV{i}", 4) for i in range(2)]; sA = [mksite(es, f"sA{i}", 1) for i in range(2)]
            sW = [mksite(es, f"sW{i}", 2) for i in range(2)]
            gpostm = sb(es, "gpostm", [128, D]); gv_t = sb(es, "gv_t", [128, 512]); bglu_t = sb(es, "bglu_t", [128, 512])
            E.dma('sp', 'gpost', gpostm[:], W["g_mix_post"].ap().partition_broadcast(128), writes=['gpostm'])
            E.dma('sp', 'gpost', gv_t[:], W["gmlp_g_v"].ap().partition_broadcast(128), writes=['gv_t'])
            E.dma('sp', 'gpost', bglu_t[:], W["s5_b_glu"].ap().partition_broadcast(128), writes=['bglu_t'])
            E.dma('sp', 'kt', Kt[:], kt_s[:], reads=['kt_s'], writes=['Kt'])
            E.dma('sp', 'cm', Cm[:], cm_s[:], reads=['cm_s'], writes=['Cm'])
            winv = W["w_in"].ap().rearrange("(k p) n -> p k n", p=128)
            E.dma('pool', 'wina', wina[:], winv[:, :, 0:1024], writes=['wina'])
            E.dma('pool', 'wout', wout[:], W["w_out"].ap().rearrange("(k p) n -> p k n", p=128), writes=['wout'])
            E.dma('pool', 'wglu', wglu[:], W["s5_w_glu"].ap().rearrange("(k p) n -> p k n", p=128), writes=['wglu'])
            oi = 0
            for blk in range(nblk):
                tokb = blk * ntokb
                E.dma('sp', 'ubm', Ub[:, :, 0:ncb], u_s[blk][:, :, 0:ncb], reads=['u_s'], writes=['Ubm'])
                E.dma('sp', 'ntl', nT[:, :, 0:ntokb], nt_s[blk][:, :, 0:ntokb], reads=['nt_s'], writes=['nT'])
                for g4 in range(8):
                    b_ = ps[g4 % 2]
                    for gg in range(4):
                        g = g4 * 4 + gg
                        E.op('pe', lambda e: e.matmul(b_[0:ncb, gg * 128:(gg + 1) * 128], lhsT=Ub[:, g, 0:ncb], rhs=Kt[:, g, :],
                                                      start=True, stop=False), reads=['Ubm', 'Kt'], writes=[PSN[g4 % 2]])
                        E.op('pe', lambda e: e.matmul(b_[0:ncb, gg * 128:(gg + 1) * 128], lhsT=Pb[:, g, blk * ncb:(blk + 1) * ncb],
                                                      rhs=Cm[:, g, :], start=False, stop=True), reads=['Pb', 'Cm'], writes=[PSN[g4 % 2]])
                    E.op('act', lambda e: e.activation(
                        out=Y[0:ncb].rearrange("c t (g n) -> c t g n", g=32)[:, :, g4 * 4:(g4 + 1) * 4, :],
                        in_=b_[0:ncb, :].rearrange("c (g t n) -> c t g n", g=4, t=8), func=AF.Gelu_apprx_tanh),
                        reads=[PSN[g4 % 2]], writes=[f'Yg{g4}'] + [f'Yj{j}' for j in range(8)])
                Yall = [f'Yg{i}' for i in range(8)]
                def glu_A(j):
                    p = j % 2
                    pb_ = pTs[p]; pbn = pTn[p]
                    E.op('dve', lambda e: e.tensor_copy(out=ybf[p][0:ncb, :], in_=Y[0:ncb, j, :]), reads=Yall + [f'Yj{j}'], writes=[f'ybf{p}'])
                    for kt in range(4):
                        E.op('pe', lambda e: e.transpose(out=pb_[:, kt * 128:kt * 128 + ncb], in_=ybf[p][0:ncb, kt * 128:(kt + 1) * 128],
                                                         identity=ident_b[0:ncb, 0:ncb]), reads=[f'ybf{p}', 'ident_b'], writes=[pbn])
                    E.op('act', lambda e: e.copy(out=y2T[p][:, :, 0:ncb], in_=pb_[:, 0:512].rearrange("p (k c) -> p k c", k=4)[:, :, 0:ncb]),
                         reads=[pbn], writes=[f'y2T{p}'])

                def glu_B(j):
                    p = j % 2
                    for kt in range(4):
                        E.op('pe', lambda e: e.matmul(ps[2 + p][0:ncb, :], lhsT=y2T[p][:, kt, 0:ncb], rhs=wglu[:, kt, :],
                                                      start=(kt == 0), stop=(kt == 3)), reads=[f'y2T{p}', 'wglu'], writes=[PSN[2 + p]])
                    E.op('dve', lambda e: e.tensor_tensor(out=gt[p][0:ncb, :], in0=ps[2 + p][0:ncb, :], in1=bglu_t[0:ncb, :], op=ALU.add),
                         reads=[PSN[2 + p], 'bglu_t'], writes=[f'gt{p}'])
                    E.op('act', lambda e: e.activation(out=gt[p][0:ncb, :], in_=gt[p][0:ncb, :], func=AF.Tanh, scale=0.5),
                         reads=[f'gt{p}'], writes=[f'gt{p}'])
                    E.op('dve', lambda e: e.scalar_tensor_tensor(out=Y[0:ncb, j, :], in0=gt[p][0:ncb, :], scalar=1.0, in1=Y[0:ncb, j, :],
                                                                 op0=ALU.add, op1=ALU.mult), reads=[f'gt{p}', f'Yj{j}'] + Yall, writes=[f'Yj{j}'])
                    E.op('act', lambda e: e.activation(out=junk[0:ncb, 0:512], in_=Y[0:ncb, j, :], func=AF.Square,
                                                       accum_out=sB.ss[0:ncb, j:j + 1]), reads=[f'Yj{j}'], writes=['junk', 'sBss'])
                for i in range(9):
                    if i < 8:
                        glu_A(i)
                    if i >= 1:
                        glu_B(i - 1)
                rsqrt_site(sB, ncb, 0, 8, 1.0 / (4 * 512))
                for j in range(8):
                    p = j % 2
                    pb_ = pTs[p]; pbn = pTn[p]
                    E.op('act', lambda e: e.mul(out=ybf[p][0:ncb, :], in_=Y[0:ncb, j, :], mul=sB.rs[0:ncb, j:j + 1]),
                         reads=[f'Yj{j}', 'sBrs'], writes=[f'ybf{p}'])
                    for kt in range(4):
                        E.op('pe', lambda e: e.transpose(out=pb_[:, kt * 128:kt * 128 + ncb], in_=ybf[p][0:ncb, kt * 128:(kt + 1) * 128],
                                                         identity=ident_b[0:ncb, 0:ncb]), reads=[f'ybf{p}', 'ident_b'], writes=[pbn])
                    E.op('dve', lambda e: e.tensor_tensor(out=mixT[:, 4:8, j:ntokb:8],
                                                          in0=pb_[:, 0:512].rearrange("p (k c) -> p k c", k=4)[:, :, 0:ncb],
                                                          in1=gc_b[:].unsqueeze(2).to_broadcast([128, 4, ncb]), op=ALU.mult),
                         reads=[pbn], writes=['mixTb'])
                def gm_A(st):
                    p = st % 2
                    tsl = slice(st * nt, (st + 1) * nt)
                    pz = [ps[2 * p], ps[2 * p + 1]]; pzn = [PSN[2 * p], PSN[2 * p + 1]]
                    for half in range(2):
                        for k in range(8):
                            E.op('pe', lambda e: e.matmul(pz[half][0:nt, :], lhsT=nT[:, k, tsl], rhs=wina[:, k, half * 512:(half + 1) * 512],
                                                          start=(k == 0), stop=(k == 7)), reads=['nT', 'wina'], writes=[pzn[half]])
                    E.op('act', lambda e: e.activation(out=uu[p][0:nt, :], in_=pz[0][0:nt, :], func=AF.Gelu_apprx_tanh), reads=[pzn[0]], writes=[f'uu{p}'])
                    E.op('act', lambda e: e.activation(out=vv[p][0:nt, :], in_=pz[1][0:nt, :], func=AF.Gelu_apprx_tanh), reads=[pzn[1]], writes=[f'vv{p}'])
                    for h in range(4):
                        E.op('act', lambda e: e.activation(out=junk[0:nt, 0:128], in_=vv[p][0:nt, h * 128:(h + 1) * 128], func=AF.Square,
                                                           accum_out=sV[p].ss[0:nt, h:h + 1]), reads=[f'vv{p}'], writes=['junk', sV[p].n + 'ss'])
                    rsqrt_site(sV[p], nt, 0, 4, 1.0 / 128)
                    E.op('dve', lambda e: e.tensor_tensor(out=vnf[p][0:nt, :].rearrange("t (h d) -> t h d", h=4),
                                                          in0=vv[p][0:nt, :].rearrange("t (h d) -> t h d", h=4),
                                                          in1=sV[p].rs[0:nt, 0:4].unsqueeze(2).to_broadcast([nt, 4, 128]), op=ALU.mult),
                         reads=[f'vv{p}', sV[p].n + 'rs'], writes=[f'vnf{p}'])
                    E.op('dve', lambda e: e.tensor_tensor(out=vnf[p][0:nt, :], in0=vnf[p][0:nt, :], in1=gv_t[0:nt, :], op=ALU.mult),
                         reads=[f'vnf{p}', 'gv_t'], writes=[f'vnf{p}'])
                    E.op('act', lambda e: e.copy(out=vnb[p][0:nt, :], in_=vnf[p][0:nt, :]), reads=[f'vnf{p}'], writes=[f'vnb{p}'])
                    if vs_out is not None:
                        E.dma('pool', 'vs', vs_out[tokb + st * nt:tokb + (st + 1) * nt, :], vnf[p][0:nt, :], reads=[f'vnf{p}'], writes=['vs_out'])

                def gm_B(st):
                    p = st % 2
                    psp = ps[4 + p]; pspn = PSN[4 + p]
                    for h in range(4):
                        E.op('pe', lambda e: e.matmul(psp[0:nt, h * 128:(h + 1) * 128], lhsT=wsT[0:nt, h, 0:nt],
                                                      rhs=vnb[p][0:nt, h * 128:(h + 1) * 128], start=True, stop=True),
                             reads=['wsT', f'vnb{p}'], writes=[pspn])
                    for h in range(4):
                        E.op('dve', lambda e: e.scalar_tensor_tensor(out=aa[p][0:nt, h * 128:(h + 1) * 128], in0=psp[0:nt, h * 128:(h + 1) * 128],
                                                                     scalar=bsT[0:nt, h:h + 1], in1=uu[p][0:nt, h * 128:(h + 1) * 128],
                                                                     op0=ALU.add, op1=ALU.mult), reads=[pspn, f'uu{p}'], writes=[f'aa{p}'])
                    E.op('act', lambda e: e.activation(out=junk[0:nt, 0:512], in_=aa[p][0:nt, :], func=AF.Square,
                                                       accum_out=sA[p].ss[0:nt, 0:1]), reads=[f'aa{p}'], writes=['junk', sA[p].n + 'ss'])
                    rsqrt_site(sA[p], nt, 0, 1, 1.0 / 512)
                    E.op('act', lambda e: e.mul(out=anb[p][0:nt, :], in_=aa[p][0:nt, :], mul=sA[p].rs[0:nt, 0:1]),
                         reads=[f'aa{p}', sA[p].n + 'rs'], writes=[f'anb{p}'])

                def gm_C(st):
                    p = st % 2
                    tsl = slice(st * nt, (st + 1) * nt)
                    pb_ = pTs[p]; pbn = pTn[p]
                    for kt in range(4):
                        E.op('pe', lambda e: e.transpose(out=pb_[:, kt * 128:kt * 128 + nt], in_=anb[p][0:nt, kt * 128:(kt + 1) * 128],
                                                         identity=ident_b[0:nt, 0:nt]), reads=[f'anb{p}', 'ident_b'], writes=[pbn])
                    E.op('dve', lambda e: e.tensor_tensor(out=mixT[:, 0:4, tsl],
                                                          in0=pb_[:, 0:512].rearrange("p (k c) -> p k c", k=4)[:, :, 0:nt],
                                                          in1=gc_a[:].unsqueeze(2).to_broadcast([128, 4, nt]), op=ALU.mult),
                         reads=[pbn], writes=['mixTa'])
                for i in range(nst + 2):
                    if i < nst:
                        gm_A(i)
                    if 0 <= i - 1 < nst:
                        gm_B(i - 1)
                    if 0 <= i - 2 < nst:
                        gm_C(i - 2)
                for st in range(nst):
                    p = st % 2
                    tsl = slice(st * nt, (st + 1) * nt)
                    hs_ = hsl[p]; hnm = f'hsl{p}'
                    E.dma('sp', f'h{p}', hs_[0:nt, :], src[tokb + st * nt:tokb + (st + 1) * nt, :], writes=[hnm])
                    pw = [ps[2 * p], ps[2 * p + 1]]; pwn = [PSN[2 * p], PSN[2 * p + 1]]
                    for half in range(2):
                        for k in range(8):
                            E.op('pe', lambda e: e.matmul(pw[half][0:nt, :], lhsT=mixT[:, k, tsl], rhs=wout[:, k, half * 512:(half + 1) * 512],
                                                          start=(k == 0), stop=(k == 7)), reads=['mixTa', 'mixTb', 'wout'], writes=[pwn[half]])
                        E.op('act', lambda e: e.activation(out=junk[0:nt, 0:512], in_=pw[half][0:nt, :], func=AF.Square,
                                                           accum_out=sW[p].ss[0:nt, half:half + 1]), reads=[pwn[half]], writes=['junk', sW[p].n + 'ss'])
                    E.op('dve', lambda e: e.tensor_tensor(out=sW[p].ss[0:nt, 0:1], in0=sW[p].ss[0:nt, 0:1], in1=sW[p].ss[0:nt, 1:2], op=ALU.add),
                         reads=[sW[p].n + 'ss'], writes=[sW[p].n + 'ss'])
                    rsqrt_site(sW[p], nt, 0, 1, 1.0 / D)
                    o_ = ot[oi % 2]; on = f'otm{oi % 2}'; oi += 1
                    for half in range(2):
                        E.op('dve', lambda e: e.scalar_tensor_tensor(out=o_[0:nt, half * 512:(half + 1) * 512], in0=pw[half][0:nt, :],
                                                                     scalar=sW[p].rs[0:nt, 0:1], in1=gpostm[0:nt, half * 512:(half + 1) * 512],
                                                                     op0=ALU.mult, op1=ALU.mult), reads=[pwn[half], sW[p].n + 'rs', 'gpostm'], writes=[on])
                    E.op('pool', lambda e: e.tensor_tensor(out=o_[0:nt, :], in0=o_[0:nt, :], in1=hs_[0:nt, :], op=ALU.add),
                         reads=[on, hnm], writes=[on])
                    E.dma('pool', on, dst[tokb + st * nt:tokb + (st + 1) * nt, :], o_[0:nt, :], reads=[on], writes=['dst'])
            E.barrier()

    def state_out(o_re, o_im):
        E.dma('pool', 'sto', o_re.ap().rearrange("g p -> p g"), CAR[0:64, :], reads=['CAR'], writes=['sto'], **NCD)
        E.dma('pool', 'sto', o_im.ap().rearrange("g p -> p g"), CAR[64:128, :], reads=['CAR'], writes=['sto'], **NCD)

    def run_segment(src, dst, ntok, nt, nblk, ncb, vs_out=None, lite=False):
        ffn_phase(src, h_s.ap(), ntok, nt, 0, "w_ffn1_down", gc_f1, "g_ffn1_post", xu=ncb, save_nT=not lite)
        if lite:
            scan_phase(nblk, ncb, None, lite=True)
            return
        with ExitStack() as es:
            Pb = sb(es, "Pb", [128, 32, NMC], BF16)
            scan_phase(nblk, ncb, Pb)
            mixer_phase(h_s.ap(), h2_s.ap(), nblk, ncb, nt, Pb, vs_out)
        ffn_phase(h2_s.ap(), dst, ntok, nt, 1, "w_ffn2_down", gc_f2, "g_ffn2_post")

    E.op('dve', lambda e: e.memset(CAR[:], 0.0), writes=['CAR'])
    if NPRE > 0:
        ffn_phase(xpre.ap(), None, NPRE * SEG, 128, 0, "w_ffn1_down", gc_f1, "g_ffn1_post", xu=128)
        lite_scan_merged(NPRE, NBLK, 128)
    for sg in range(NSEG):
        run_segment(xp.ap()[sg * SEG:(sg + 1) * SEG, :], yp.ap()[sg * SEG:(sg + 1) * SEG, :], SEG, 128, NBLK, 128)
    state_out(sp_re, sp_im)
    E.barrier()
    E.dma('sp', 'setup', CAR[0:64, :], st_re.ap().rearrange("g p -> p g"), writes=['CAR'], **NCD)
    E.dma('sp', 'setup', CAR[64:128, :], st_im.ap().rearrange("g p -> p g"), writes=['CAR'], **NCD)
    run_segment(xs.ap(), ys.ap(), 16, 16, 1, 2, vs_out=vs.ap())
    state_out(ss_re, ss_im)
    E.barrier()
    top.close()
    return nc


_WNAMES = ["g_ffn1_pre", "w_ffn1_gate", "w_ffn1_up", "w_ffn1_down", "g_ffn1_post", "g_mix_pre", "w_in", "gmlp_g_v",
           "gmlp_w_s", "gmlp_b_s", "s5_lam_re", "s5_lam_im", "s5_log_dt", "s5_b_re", "s5_b_im", "s5_c_re", "s5_c_im",
           "s5_d", "s5_w_glu", "s5_b_glu", "g_a_out", "g_b_out", "w_out", "g_mix_post", "g_ffn2_pre", "w_ffn2_gate",
           "w_ffn2_up", "w_ffn2_down", "g_ffn2_post"]


def kernel(**inputs):
    x_prompt = np.asarray(inputs["x_prompt"], dtype=np.float32)
    x_sample = np.asarray(inputs["x_sample"], dtype=np.float32)
    B, LT, _ = x_prompt.shape
    CPS = 8 // B
    L = LT // CPS
    NBLK = min(4, L // 1024)
    LPRE = (CPS - 1) * L
    nc = build_nc(L, NBLK, LPRE)
    wmap = {nm: np.ascontiguousarray(np.asarray(inputs[nm], dtype=np.float32)[0]) for nm in _WNAMES}
    sre = np.asarray(inputs["state_ssm_re"], dtype=np.float32)[0]
    sim = np.asarray(inputs["state_ssm_im"], dtype=np.float32)[0]
    in_maps = []
    for c in range(8):
        m = dict(wmap)
        sq, i = c // CPS, c % CPS
        m["xp"] = np.ascontiguousarray(x_prompt[sq, i * L:(i + 1) * L])
        pre = np.zeros((max(LPRE, 1), D), np.float32)
        if i > 0:
            pre[LPRE - i * L:] = x_prompt[sq, 0:i * L]
        m["xpre"] = pre
        m["xs"] = np.ascontiguousarray(x_sample[c])
        m["st_re"] = np.ascontiguousarray(sre[c]); m["st_im"] = np.ascontiguousarray(sim[c])
        in_maps.append(m)
    res = run_bass_kernel_spmd(nc, in_maps, core_ids=list(range(8)))
    R = res.results
    y_p = np.stack([np.concatenate([R[sq * CPS + i]["yp"] for i in range(CPS)], axis=0) for sq in range(B)])
    y_s = np.stack([R[c]["ys"] for c in range(8)])
    spr = np.stack([R[sq * CPS + CPS - 1]["sp_re"] for sq in range(B)])[None]
    spi = np.stack([R[sq * CPS + CPS - 1]["sp_im"] for sq in range(B)])[None]
    ssr = np.stack([R[c]["ss_re"] for c in range(8)])[None]
    ssi = np.stack([R[c]["ss_im"] for c in range(8)])[None]
    v_s = np.stack([R[c]["vs"] for c in range(8)])[None]
    return (y_p.astype(np.float32), y_s.astype(np.float32), spr.astype(np.float32), spi.astype(np.float32),
            ssr.astype(np.float32), ssi.astype(np.float32), v_s.astype(np.float32))
```

```python
import os
from contextlib import ExitStack
import numpy as np
import concourse.bass as bass
import concourse.mybir as mybir
from concourse.bass_utils import run_bass_kernel_spmd

F32 = mybir.dt.float32
BF16 = mybir.dt.bfloat16
AF = mybir.ActivationFunctionType
ALU = mybir.AluOpType

D = 1024
DFF = 2816
NFC = 22
NPIECE = 11
EPS = 1e-6
PI = float(np.pi)


class Emit:
    def __init__(self, nc):
        self.nc = nc
        self.eng = {'pe': nc.tensor, 'act': nc.scalar, 'dve': nc.vector,
                    'pool': nc.gpsimd, 'sp': nc.sync}
        self.sem = {}
        self.cnt = {}
        for e in ['pe', 'act', 'dve', 'pool']:
            self.sem[e] = nc.alloc_semaphore(name=f"s_{e}")
            self.cnt[e] = 0
        self.lastw = {}
        self.readers = {}
        self.waited = {}

    def _sem(self, key):
        if key not in self.sem:
            self.sem[key] = self.nc.alloc_semaphore(name=("s_" + key).replace(':', '_'))
            self.cnt[key] = 0
        return self.sem[key]

    def _wait(self, engine, k, v):
        if v <= 0 or self.waited.get((engine, k), 0) >= v:
            return
        self.eng[engine].wait_ge(self._sem(k), v)
        self.waited[(engine, k)] = v

    def _deps(self, engine, reads, writes):
        toks = {}

        def add(t):
            if t is not None and toks.get(t[0], 0) < t[1]:
                toks[t[0]] = t[1]
        for r in reads:
            add(self.lastw.get(r))
        for w in writes:
            add(self.lastw.get(w))
            for t in self.readers.get(w, ()):
                add(t)
        for k, v in toks.items():
            if k == engine and engine == 'pe':
                continue
            self._wait(engine, k, v)

    def _record(self, tok, reads, writes):
        for r in reads:
            lst = self.readers.setdefault(r, [])
            lst[:] = [t for t in lst if t[0] != tok[0]]
            lst.append(tok)
        for w in writes:
            self.lastw[w] = tok
            self.readers[w] = []

    def op(self, engine, fn, reads=(), writes=()):
        self._deps(engine, reads, writes)
        inst = fn(self.eng[engine])
        self.cnt[engine] += 1
        inst.then_inc(self.sem[engine], 1)
        self._record((engine, self.cnt[engine]), reads, writes)

    def dma(self, queue, slot, out, in_, reads=(), writes=(), **kw):
        if slot in ('setup', 'cast', 'gpost', 'sto'):
            self._rr = getattr(self, '_rr', 0) + 1
            slot = f"{slot}{self._rr % 6}"
        key = 'dma:' + slot
        sem = self._sem(key)
        self._wait(queue, key, self.cnt[key])
        self._deps(queue, reads, writes)
        inst = self.eng[queue].dma_start(out=out, in_=in_, **kw)
        self.cnt[key] += 16
        inst.then_inc(sem, 16)
        self._record((key, self.cnt[key]), reads, writes)

    def barrier(self):
        for e in ['pe', 'act', 'dve', 'pool', 'sp']:
            for k, v in self.cnt.items():
                if k == e and e == 'pe':
                    continue
                self._wait(e, k, v)
        self.lastw.clear()
        self.readers.clear()


def build_nc(L, NBLK, LPRE=0):
    SEG = NBLK * 1024
    NSEG = L // SEG
    NPRE = LPRE // SEG
    NMC = NBLK * 128
    nc = bass.Bass("TRN2", target_bir_lowering=False)
    E = Emit(nc)

    def din(name, shape):
        return nc.dram_tensor(name, list(shape), F32, kind="ExternalInput")

    def dout(name, shape):
        return nc.dram_tensor(name, list(shape), F32, kind="ExternalOutput")

    xp = din("xp", [L, D]); xs = din("xs", [16, D])
    xpre = din("xpre", [max(LPRE, 1), D])
    st_re = din("st_re", [32, 64]); st_im = din("st_im", [32, 64])
    W = {}
    for nm, shp in [("g_ffn1_pre", [D]), ("w_ffn1_gate", [D, DFF]), ("w_ffn1_up", [D, DFF]), ("w_ffn1_down", [DFF, D]),
                    ("g_ffn1_post", [D]), ("g_mix_pre", [D]), ("w_in", [D, 1536]), ("gmlp_g_v", [512]),
                    ("gmlp_w_s", [4, 128, 128]), ("gmlp_b_s", [4, 128]), ("s5_lam_re", [32, 64]), ("s5_lam_im", [32, 64]),
                    ("s5_log_dt", [32]), ("s5_b_re", [32, 64, 16]), ("s5_b_im", [32, 64, 16]), ("s5_c_re", [32, 16, 64]),
                    ("s5_c_im", [32, 16, 64]), ("s5_d", [512]), ("s5_w_glu", [512, 512]), ("s5_b_glu", [512]),
                    ("g_a_out", [512]), ("g_b_out", [512]), ("w_out", [D, D]), ("g_mix_post", [D]),
                    ("g_ffn2_pre", [D]), ("w_ffn2_gate", [D, DFF]), ("w_ffn2_up", [D, DFF]), ("w_ffn2_down", [DFF, D]),
                    ("g_ffn2_post", [D])]:
        W[nm] = din(nm, shp)
    yp = dout("yp", [L, D]); ys = dout("ys", [16, D])
    sp_re = dout("sp_re", [32, 64]); sp_im = dout("sp_im", [32, 64])
    ss_re = dout("ss_re", [32, 64]); ss_im = dout("ss_im", [32, 64])
    vs = dout("vs", [16, 512])

    def dscr(name, shape, dt):
        return nc.dram_tensor(name, list(shape), dt, kind="Internal")
    wg_s = [dscr(f"wg_s{i}", [NPIECE, 128, 8, 256], BF16) for i in range(2)]
    wu_s = [dscr(f"wu_s{i}", [NPIECE, 128, 8, 256], BF16) for i in range(2)]
    h_s = dscr("h_s", [SEG, D], F32)
    h2_s = dscr("h2_s", [SEG, D], F32)
    u_s = dscr("u_s", [max(NBLK, NPRE * NBLK), 128, 32, 128], BF16)
    lam_s = dscr("lam_s", [10, 128, 32, 128], F32)
    nt_s = dscr("nt_s", [NBLK, 128, 8, 1024], BF16)
    wst_s = dscr("wst_s", [128, 32, 128], BF16)
    kt_s = dscr("kt_s", [128, 32, 128], BF16)
    cm_s = dscr("cm_s", [128, 32, 128], BF16)

    top = ExitStack()

    _uid = [0]

    def sb(es, name, shape, dt=F32):
        _uid[0] += 1
        return es.enter_context(nc.sbuf_tensor(f"{name}_{_uid[0]}", list(shape), dt))

    NCD = dict(allow_slow_non_contiguous=True)
    ps = [top.enter_context(nc.psum_tensor(f"ps{i}", [128, 512], F32)) for i in range(6)]
    pT = top.enter_context(nc.psum_tensor("pT", [128, 1024], BF16))
    pTb = top.enter_context(nc.psum_tensor("pTb", [128, 1024], BF16))
    K_DUALT = os.environ.get("K_DUALT", "1") == "1"; K_ROT = os.environ.get("K_ROT", "1") == "1"; K_SPLIT = os.environ.get("K_SPLIT", "1") == "1"
    pTs = [pT, pTb] if K_DUALT else [pT, pT]; pTn = ['pT', 'pTb'] if K_DUALT else ['pT', 'pT']
    PSN = [f"ps{i}" for i in range(6)]

    ident_f = sb(top, "ident_f", [128, 128]); ident_b = sb(top, "ident_b", [128, 128], BF16)
    gc_f1 = sb(top, "gc_f1", [128, 8]); gc_mx = sb(top, "gc_mx", [128, 8]); gc_f2 = sb(top, "gc_f2", [128, 8])
    gc_a = sb(top, "gc_a", [128, 4]); gc_b = sb(top, "gc_b", [128, 4])
    bsT = sb(top, "bsT", [128, 4]); wsT = sb(top, "wsT", [128, 4, 128], BF16)
    CAR = sb(top, "CAR", [128, 32])
    ss = sb(top, "ss", [128, 16]); rs = sb(top, "rs", [128, 16]); rtmp = sb(top, "rtmp", [128, 16])
    junk = sb(top, "junk", [128, D], BF16)

    def rsqrt_cols(n, ncols, scale, src='ss'):
        s_ = ss[0:n, 0:ncols]; r_ = rs[0:n, 0:ncols]; t_ = rtmp[0:n, 0:ncols]
        ri = rs.bitcast(mybir.dt.int32)[0:n, 0:ncols]; si = ss.bitcast(mybir.dt.int32)[0:n, 0:ncols]
        E.op('dve', lambda e: e.tensor_scalar(out=s_, in0=s_, scalar1=scale, scalar2=EPS, op0=ALU.mult, op1=ALU.add),
             reads=['ss'], writes=['ss'])
        E.op('dve', lambda e: e.tensor_scalar(out=ri, in0=si, scalar1=1, scalar2=None, op0=ALU.arith_shift_right),
             reads=['ss'], writes=['rs'])
        E.op('dve', lambda e: e.tensor_scalar(out=ri, in0=ri, scalar1=-1, scalar2=0x5f3759df, op0=ALU.mult, op1=ALU.add),
             reads=['rs'], writes=['rs'])
        for _ in range(3):
            E.op('dve', lambda e: e.scalar_tensor_tensor(out=t_, in0=r_, scalar=-0.5, in1=r_, op0=ALU.mult, op1=ALU.mult),
                 reads=['rs'], writes=['rtmp'])
            E.op('dve', lambda e: e.tensor_tensor(out=t_, in0=t_, in1=s_, op=ALU.mult), reads=['rtmp', 'ss'], writes=['rtmp'])
            E.op('dve', lambda e: e.scalar_tensor_tensor(out=r_, in0=t_, scalar=1.5, in1=r_, op0=ALU.add, op1=ALU.mult),
                 reads=['rtmp', 'rs'], writes=['rs'])

    E.op('pool', lambda e: e.memset(ident_f[:], 1.0), writes=['ident_f'])
    E.op('pool', lambda e: e.affine_select(out=ident_f[:], in_=ident_f[:], pattern=[[-1, 128]], compare_op=ALU.is_equal,
                                           fill=0.0, base=0, channel_multiplier=1), reads=['ident_f'], writes=['ident_f'])
    E.op('dve', lambda e: e.tensor_copy(out=ident_b[:], in_=ident_f[:]), reads=['ident_f'], writes=['ident_b'])
    for t, nm, sc in [(gc_f1, "g_ffn1_pre", 1.0), (gc_mx, "g_mix_pre", 1.0), (gc_f2, "g_ffn2_pre", 1.0),
                      (gc_a, "g_a_out", 1.0), (gc_b, "g_b_out", 0.5)]:
        E.dma('sp', 'setup', t[:], W[nm].ap().rearrange("(k p) -> p k", p=128), writes=[nm], **NCD)
        if sc != 1.0:
            E.op('dve', lambda e: e.tensor_scalar(out=t[:], in0=t[:], scalar1=sc, scalar2=None, op0=ALU.mult),
                 reads=[nm], writes=[nm])
    E.dma('sp', 'setup', bsT[:], W["gmlp_b_s"].ap().rearrange("h t -> t h"), writes=['bsT'], **NCD)

    with ExitStack() as es:
        ws_nat = sb(es, "ws_nat", [128, 4, 128]); wsT_f = sb(es, "wsT_f", [128, 4, 128])
        E.dma('sp', 'setup', ws_nat[:], W["gmlp_w_s"].ap().rearrange("h t s -> t h s"), writes=['ws_nat'])
        E.op('pool', lambda e: e.affine_select(out=ws_nat[:], in_=ws_nat[:], pattern=[[0, 4], [-1, 128]], compare_op=ALU.is_ge,
                                               fill=0.0, base=0, channel_multiplier=1), reads=['ws_nat'], writes=['ws_nat'])
        for h in range(4):
            E.op('pe', lambda e: e.transpose(out=ps[5][:, h * 128:(h + 1) * 128], in_=ws_nat[:, h, :], identity=ident_f[:]),
                 reads=['ws_nat', 'ident_f'], writes=['ps5'])
        E.op('dve', lambda e: e.tensor_copy(out=wsT_f[:], in_=ps[5][:].rearrange("p (h t) -> p h t", h=4)),
             reads=['ps5'], writes=['wsT_f'])
        E.op('dve', lambda e: e.tensor_copy(out=wsT[:], in_=wsT_f[:]), reads=['wsT_f'], writes=['wsT'])

        def t32(name):
            return sb(es, name, [128, 32])
        LR = t32("LR"); LI = t32("LI"); DT = t32("DT"); A_ = t32("A_"); B_ = t32("B_"); MAG = t32("MAG")
        BR = t32("BR"); SINB = t32("SINB"); COSB = t32("COSB"); T1 = t32("T1"); T2 = t32("T2")
        La = t32("La"); Lb = t32("Lb"); CA_ = t32("CA_"); CB_ = t32("CB_"); DEN = t32("DEN"); NA = t32("NA")
        PWA = sb(es, "PWA", [128, 9, 32]); PWB = sb(es, "PWB", [128, 9, 32])
        QA = sb(es, "QA", [128, 10, 32]); QB = sb(es, "QB", [128, 10, 32]); QBs = sb(es, "QBs", [128, 10, 32])
        sgn = sb(es, "sgn", [128, 2])
        Jm = sb(es, "Jm", [128, 128]); J2 = sb(es, "J2", [128, 128])
        maskj = sb(es, "maskj", [128, 8]); DD = sb(es, "DD", [128, 32])

        E.op('pool', lambda e: e.memset(maskj[:], 1.0), writes=['maskj'])
        E.op('pool', lambda e: e.affine_select(out=maskj[:], in_=maskj[:], pattern=[[-16, 8]], compare_op=ALU.is_ge, fill=0.0,
                                               base=0, channel_multiplier=1), reads=['maskj'], writes=['maskj'])
        E.op('pool', lambda e: e.affine_select(out=maskj[:], in_=maskj[:], pattern=[[16, 8]], compare_op=ALU.is_ge, fill=0.0,
                                               base=15, channel_multiplier=-1), reads=['maskj'], writes=['maskj'])
        E.op('pool', lambda e: e.memset(Jm[:], 1.0), writes=['Jm']); E.op('pool', lambda e: e.memset(J2[:], 1.0), writes=['J2'])
        E.op('pool', lambda e: e.affine_select(out=Jm[:], in_=Jm[:], pattern=[[1, 128]], compare_op=ALU.is_equal, fill=0.0,
                                               base=-64, channel_multiplier=-1), reads=['Jm'], writes=['Jm'])
        E.op('pool', lambda e: e.affine_select(out=J2[:], in_=J2[:], pattern=[[-1, 128]], compare_op=ALU.is_equal, fill=0.0,
                                               base=-64, channel_multiplier=1), reads=['J2'], writes=['J2'])
        for i, pre in enumerate(["w_ffn1", "w_ffn2"]):
            for dst, nm in [(wg_s[i], pre + "_gate"), (wu_s[i], pre + "_up")]:
                src = W[nm].ap().rearrange("(k p) (c f) -> c p k f", p=128, f=256)
                for c in range(NPIECE):
                    E.dma('pool', 'cast', dst[c], src[c], writes=[f"{nm}_s"])
        def V(fn, reads, writes):
            E.op('dve', fn, reads=reads, writes=writes)

        def tt(o, a, b, op, rn, wn):
            V(lambda e: e.tensor_tensor(out=o, in0=a, in1=b, op=op), rn, wn)

        for half in range(2):
            E.dma('sp', 'setup', LR[half * 64:(half + 1) * 64, :], W["s5_lam_re"].ap().rearrange("g p -> p g"), writes=['LR'], **NCD)
            E.dma('sp', 'setup', LI[half * 64:(half + 1) * 64, :], W["s5_lam_im"].ap().rearrange("g p -> p g"), writes=['LI'], **NCD)
        E.dma('sp', 'setup', DT[:], W["s5_log_dt"].ap().partition_broadcast(128), writes=['DT'])
        E.op('act', lambda e: e.activation(out=DT[:], in_=DT[:], func=AF.Exp), reads=['DT'], writes=['DT'])
        tt(A_[:], LR[:], DT[:], ALU.mult, ['LR', 'DT'], ['A_'])
        tt(B_[:], LI[:], DT[:], ALU.mult, ['LI', 'DT'], ['B_'])
        E.op('act', lambda e: e.activation(out=MAG[:], in_=A_[:], func=AF.Exp), reads=['A_'], writes=['MAG'])

        def sin_of(dst, src, shift, nm):
            V(lambda e: e.tensor_scalar(out=BR[:], in0=src[:], scalar1=shift, scalar2=None, op0=ALU.add), [nm], ['BR'])
            V(lambda e: e.tensor_copy(out=T2[:], in_=BR[:]), ['BR'], ['T2'])
            for kk in range(5):
                thr = (2 * kk + 1) * PI
                V(lambda e: e.tensor_scalar(out=T1[:], in0=T2[:], scalar1=thr, scalar2=-2 * PI, op0=ALU.is_gt, op1=ALU.mult),
                  ['T2'], ['T1'])
                tt(BR[:], BR[:], T1[:], ALU.add, ['BR', 'T1'], ['BR'])
            V(lambda e: e.tensor_scalar(out=BR[:], in0=BR[:], scalar1=-PI, scalar2=PI, op0=ALU.max, op1=ALU.min), ['BR'], ['BR'])
            E.op('act', lambda e: e.activation(out=dst[:], in_=BR[:], func=AF.Sin), reads=['BR'], writes=[nm + '_sin'])
        sin_of(SINB, B_, 0.0, 'B_')
        E.barrier()
        sin_of(COSB, B_, PI / 2, 'B_')
        E.barrier()
        tt(La[:], MAG[:], COSB[:], ALU.mult, [], ['La'])
        tt(Lb[:], MAG[:], SINB[:], ALU.mult, [], ['Lb'])

        def cmul(za, zb, xa, xb, ya, yb):
            tt(T1[:], xa, ya, ALU.mult, ['cm'], ['cm']); tt(T2[:], xb, yb, ALU.mult, ['cm'], ['cm'])
            tt(za, T1[:], T2[:], ALU.subtract, ['cm'], ['cm'])
            tt(T1[:], xa, yb, ALU.mult, ['cm'], ['cm']); tt(T2[:], xb, ya, ALU.mult, ['cm'], ['cm'])
            tt(zb, T1[:], T2[:], ALU.add, ['cm'], ['cm'])
        V(lambda e: e.memset(PWA[:, 0, :], 1.0), ['cm'], ['cm']); V(lambda e: e.memset(PWB[:, 0, :], 0.0), ['cm'], ['cm'])
        for k in range(1, 9):
            cmul(PWA[:, k, :], PWB[:, k, :], PWA[:, k - 1, :], PWB[:, k - 1, :], La[:], Lb[:])
        V(lambda e: e.tensor_copy(out=QA[:, 0, :], in_=PWA[:, 8, :]), ['cm'], ['cm'])
        V(lambda e: e.tensor_copy(out=QB[:, 0, :], in_=PWB[:, 8, :]), ['cm'], ['cm'])
        for l in range(1, 10):
            cmul(QA[:, l, :], QB[:, l, :], QA[:, l - 1, :], QB[:, l - 1, :], QA[:, l - 1, :], QB[:, l - 1, :])
        V(lambda e: e.memset(sgn[0:64, 0:1], -1.0), ['cm'], ['cm']); V(lambda e: e.memset(sgn[64:128, 0:1], 1.0), ['cm'], ['cm'])
        V(lambda e: e.memset(sgn[0:64, 1:2], 1.0), ['cm'], ['cm']); V(lambda e: e.memset(sgn[64:128, 1:2], -1.0), ['cm'], ['cm'])
        V(lambda e: e.tensor_scalar(out=QBs[:], in0=QB[:], scalar1=sgn[:, 1:2], scalar2=None, op0=ALU.mult), ['cm'], ['cm'])
        V(lambda e: e.tensor_scalar(out=NA[:], in0=La[:], scalar1=-1.0, scalar2=None, op0=ALU.add), ['cm'], ['cm'])
        tt(T1[:], LR[:], LR[:], ALU.mult, ['cm'], ['cm']); tt(T2[:], LI[:], LI[:], ALU.mult, ['cm'], ['cm'])
        tt(DEN[:], T1[:], T2[:], ALU.add, ['cm'], ['cm'])
        V(lambda e: e.reciprocal(out=DEN[:], in_=DEN[:]), ['cm'], ['cm'])
        tt(T1[:], NA[:], LR[:], ALU.mult, ['cm'], ['cm']); tt(T2[:], Lb[:], LI[:], ALU.mult, ['cm'], ['cm'])
        tt(CA_[:], T1[:], T2[:], ALU.add, ['cm'], ['cm']); tt(CA_[:], CA_[:], DEN[:], ALU.mult, ['cm'], ['cm'])
        tt(T1[:], Lb[:], LR[:], ALU.mult, ['cm'], ['cm']); tt(T2[:], NA[:], LI[:], ALU.mult, ['cm'], ['cm'])
        tt(CB_[:], T1[:], T2[:], ALU.subtract, ['cm'], ['cm']); tt(CB_[:], CB_[:], DEN[:], ALU.mult, ['cm'], ['cm'])
        E.barrier()

        def bc(t2d):
            return t2d.unsqueeze(2).to_broadcast([128, 32, 16])
        B1 = sb(es, "B1", [128, 32, 16]); B2 = sb(es, "B2", [128, 32, 16])
        Bst = sb(es, "Bst", [128, 32, 16]); Bsw = sb(es, "Bsw", [128, 32, 16]); TB = sb(es, "TB", [128, 32, 16])
        CAt = sb(es, "CAt", [128, 32, 16]); CBt = sb(es, "CBt", [128, 32, 16])
        bre = W["s5_b_re"].ap().rearrange("g p m -> p g m"); bim = W["s5_b_im"].ap().rearrange("g p m -> p g m")
        cre = W["s5_c_re"].ap().rearrange("g n p -> p g n"); cim = W["s5_c_im"].ap().rearrange("g n p -> p g n")
        E.dma('sp', 'setup', B1[0:64], bre, writes=['B1'], **NCD); E.dma('sp', 'setup', B1[64:128], bim, writes=['B1'], **NCD)
        E.dma('sp', 'setup', B2[0:64], bim, writes=['B2'], **NCD); E.dma('sp', 'setup', B2[64:128], bre, writes=['B2'], **NCD)
        E.dma('sp', 'setup', CAt[0:64], cre, writes=['CAt'], **NCD); E.dma('sp', 'setup', CAt[64:128], cim, writes=['CAt'], **NCD)
        E.dma('sp', 'setup', CBt[0:64], cim, writes=['CBt'], **NCD); E.dma('sp', 'setup', CBt[64:128], cre, writes=['CBt'], **NCD)
        for j in range(8):
            E.dma('sp', 'setup', DD[16 * j:16 * j + 16, :], W["s5_d"].ap().rearrange("(g m) -> m g", m=16), writes=['DD'], **NCD)
        E.barrier()
        V(lambda e: e.tensor_scalar(out=B2[:], in0=B2[:], scalar1=sgn[:, 0:1], scalar2=None, op0=ALU.mult), [], [])
        V(lambda e: e.tensor_scalar(out=CAt[:], in0=CAt[:], scalar1=sgn[:, 1:2], scalar2=None, op0=ALU.mult), [], [])
        V(lambda e: e.tensor_scalar(out=CBt[:], in0=CBt[:], scalar1=-1.0, scalar2=None, op0=ALU.mult), [], [])
        E.barrier()
        tt(Bst[:], B1[:], bc(CA_[:]), ALU.mult, ['cm'], ['cm']); tt(TB[:], B2[:], bc(CB_[:]), ALU.mult, ['cm'], ['cm'])
        tt(Bst[:], Bst[:], TB[:], ALU.add, ['cm'], ['cm'])
        tt(Bsw[:], B2[:], bc(CA_[:]), ALU.mult, ['cm'], ['cm']); tt(TB[:], B1[:], bc(CB_[:]), ALU.mult, ['cm'], ['cm'])
        tt(Bsw[:], Bsw[:], TB[:], ALU.subtract, ['cm'], ['cm'])
        E.barrier()
        W7 = sb(es, "W7", [128, 32, 8, 16]); CP = sb(es, "CP", [128, 32, 8, 16]); CPE = sb(es, "CPE", [128, 32, 8, 16])
        BRep = sb(es, "BRep", [128, 32, 8, 16])
        for j in range(8):
            tt(W7[:, :, j, :], Bst[:], bc(PWA[:, 7 - j, :]), ALU.mult, ['cm'], ['cm'])
            tt(TB[:], Bsw[:], bc(PWB[:, 7 - j, :]), ALU.mult, ['cm'], ['cm'])
            tt(W7[:, :, j, :], W7[:, :, j, :], TB[:], ALU.add, ['cm'], ['cm'])
            tt(CP[:, :, j, :], CAt[:], bc(PWA[:, j + 1, :]), ALU.mult, ['cm'], ['cm'])
            tt(TB[:], CBt[:], bc(PWB[:, j + 1, :]), ALU.mult, ['cm'], ['cm'])
            tt(CP[:, :, j, :], CP[:, :, j, :], TB[:], ALU.add, ['cm'], ['cm'])
            tt(CPE[:, :, j, :], CAt[:], bc(PWA[:, j, :]), ALU.mult, ['cm'], ['cm'])
            tt(TB[:], CBt[:], bc(PWB[:, j, :]), ALU.mult, ['cm'], ['cm'])
            tt(CPE[:, :, j, :], CPE[:, :, j, :], TB[:], ALU.add, ['cm'], ['cm'])
            V(lambda e: e.tensor_copy(out=BRep[:, :, j, :], in_=Bst[:]), ['cm'], ['cm'])
        E.barrier()
        tt(Jm[:], Jm[:], J2[:], ALU.add, ['cm'], ['cm'])
        E.barrier()
        stg = sb(es, "stg", [128, 32, 128], BF16)
        for g in range(32):
            b_ = ps[g % 4]
            E.op('pe', lambda e: e.transpose(out=b_[:, 0:128], in_=W7[:, g, :, :].rearrange("p j m -> p (j m)"), identity=ident_f[:]),
                 reads=['ident_f'], writes=[PSN[g % 4]])
            E.op('act', lambda e: e.copy(out=stg[:, g, :], in_=b_[:, 0:128]), reads=[PSN[g % 4]], writes=['stg'])
        E.dma('sp', 'setup', wst_s[:], stg[:], reads=['stg'], writes=['wst_s'])
        V(lambda e: e.tensor_copy(out=stg[:], in_=CP[:].rearrange("p g t n -> p g (t n)")), ['stg'], ['stg'])
        E.dma('sp', 'setup', cm_s[:], stg[:], reads=['stg'], writes=['cm_s'])
        KT = sb(es, "KT", [128, 8, 16])
        idv = ident_f[:].rearrange("p (t n) -> p t n", t=8)
        for g in range(32):
            b_ = ps[g % 4]
            E.op('pe', lambda e: e.matmul(b_[:, 0:128], lhsT=BRep[:, g, :, :].rearrange("p j m -> p (j m)"),
                                          rhs=CPE[:, g, :, :].rearrange("p e n -> p (e n)"), start=True, stop=True),
                 reads=[], writes=[PSN[g % 4]])
            Gv = b_[:, 0:128].rearrange("p (e n) -> p e n", e=8)
            V(lambda e: e.tensor_scalar(out=KT[:], in0=idv, scalar1=DD[:, g:g + 1], scalar2=None, op0=ALU.mult), ['KT'], ['KT'])
            for j in range(8):
                V(lambda e: e.scalar_tensor_tensor(out=KT[:, j:8, :], in0=Gv[:, 0:8 - j, :], scalar=maskj[:, j:j + 1],
                                                   in1=KT[:, j:8, :], op0=ALU.mult, op1=ALU.add), ['KT', PSN[g % 4]], ['KT'])
            V(lambda e: e.tensor_copy(out=stg[:, g, :], in_=KT[:].rearrange("p t n -> p (t n)")), ['KT', 'stg'], ['KT', 'stg'])
        E.dma('sp', 'setup', kt_s[:], stg[:], reads=['stg'], writes=['kt_s'])
        LAMt = [sb(es, f"LAMt{i}", [128, 32, 128]) for i in range(2)]
        for l in range(10):
            Lt = LAMt[l % 2]; nm = f"LAMt{l % 2}"
            for g in range(32):
                rn = f"{nm}g{g}"
                E.op('act', lambda e: e.mul(out=Lt[:, g, :], in_=ident_f[:], mul=QA[:, l, g:g + 1]),
                     reads=[], writes=[rn])
            for g in range(32):
                rn = f"{nm}g{g}"
                E.op('dve', lambda e: e.scalar_tensor_tensor(out=Lt[:, g, :], in0=Jm[:], scalar=QBs[:, l, g:g + 1], in1=Lt[:, g, :],
                                                             op0=ALU.mult, op1=ALU.add), reads=[rn], writes=[rn])
            E.dma('sp', 'setup', lam_s[l], Lt[:], reads=[f"{nm}g{g}" for g in range(32)], writes=['lam_s'])
        E.barrier()

    class Site:
        pass

    def mksite(es, name, ncols):
        st_ = Site()
        st_.ss = sb(es, name + "_ss", [128, ncols]); st_.rs = sb(es, name + "_rs", [128, ncols]); st_.rt = sb(es, name + "_rt", [128, ncols])
        st_.n = name
        return st_

    def rsqrt_site(site, n, c0, c1, scale, iters=2):
        nm = site.n
        s_ = site.ss[0:n, c0:c1]; r_ = site.rs[0:n, c0:c1]; t_ = site.rt[0:n, c0:c1]
        ri = site.rs.bitcast(mybir.dt.int32)[0:n, c0:c1]; si = site.ss.bitcast(mybir.dt.int32)[0:n, c0:c1]
        E.op('dve', lambda e: e.tensor_scalar(out=s_, in0=s_, scalar1=scale, scalar2=EPS, op0=ALU.mult, op1=ALU.add),
             reads=[nm + 'ss'], writes=[nm + 'ss'])
        E.op('dve', lambda e: e.tensor_scalar(out=ri, in0=si, scalar1=1, scalar2=None, op0=ALU.arith_shift_right),
             reads=[nm + 'ss'], writes=[nm + 'rs'])
        E.op('dve', lambda e: e.tensor_scalar(out=ri, in0=ri, scalar1=-1, scalar2=0x5f3759df, op0=ALU.mult, op1=ALU.add),
             reads=[nm + 'rs'], writes=[nm + 'rs'])
        for _ in range(iters):
            E.op('dve', lambda e: e.scalar_tensor_tensor(out=t_, in0=r_, scalar=-0.5, in1=r_, op0=ALU.mult, op1=ALU.mult),
                 reads=[nm + 'rs'], writes=[nm + 'rt'])
            E.op('dve', lambda e: e.tensor_tensor(out=t_, in0=t_, in1=s_, op=ALU.mult), reads=[nm + 'rt', nm + 'ss'], writes=[nm + 'rt'])
            E.op('dve', lambda e: e.scalar_tensor_tensor(out=r_, in0=t_, scalar=1.5, in1=r_, op0=ALU.add, op1=ALU.mult),
                 reads=[nm + 'rt', nm + 'rs'], writes=[nm + 'rs'])

    def transpose8(srcb, srcname, nt_, dstT, dstname, col0, gcol):
        for hf in range(2):
            pb_ = pTs[hf]
            for kk in range(4):
                k = hf * 4 + kk
                E.op('pe', lambda e: e.transpose(out=pb_[:, kk * 128:kk * 128 + nt_],
                                                 in_=srcb[0:nt_, k * 128:(k + 1) * 128], identity=ident_b[0:nt_, 0:nt_]),
                     reads=[srcname, 'ident_b'], writes=[pTn[hf]])
            E.op('dve', lambda e: e.tensor_tensor(out=dstT[:, hf * 4:(hf + 1) * 4, col0:col0 + nt_],
                                                  in0=pb_[:, 0:512].rearrange("p (k t) -> p k t", k=4)[:, :, 0:nt_],
                                                  in1=gcol[:, hf * 4:(hf + 1) * 4].unsqueeze(2).to_broadcast([128, 4, nt_]), op=ALU.mult),
                 reads=[pTn[hf]], writes=[dstname])

    def ffn_phase(src, dst, ntok, nt, wi, wd_name, gcol, gpost_name, xu=None, save_nT=False):
        with ExitStack() as es:
            wd = sb(es, "wd", [128, NFC, D], BF16)
            gpost_t = sb(es, "gpost_t", [128, D])
            E.dma('sp', 'gpost', gpost_t[:], W[gpost_name].ap().partition_broadcast(128), writes=['gpost_t'])
            E.op('dve', lambda e: e.tensor_scalar(out=gpost_t[:], in0=gpost_t[:], scalar1=0.5, scalar2=None, op0=ALU.mult),
                 reads=['gpost_t'], writes=['gpost_t'])
            wgb = [sb(es, f"wgb{i}", [128, 8, 256], BF16) for i in range(2)]
            wub = [sb(es, f"wub{i}", [128, 8, 256], BF16) for i in range(2)]
            xsl = [sb(es, f"xsl{i}", [128, D]) for i in range(8)]
            xn = [sb(es, f"xn{i}", [128, D], BF16) for i in range(4)]
            xnT = [sb(es, f"xnT{i}", [128, 8, 512], BF16) for i in range(2)]
            hid = sb(es, "hid", [128, NFC, 512], BF16)
            sg = [sb(es, f"sg{i}", [128, 512]) for i in range(2)]
            ot = [sb(es, f"ot{i}", [128, D]) for i in range(2)]
            sH = mksite(es, "sH", 4); sP = [mksite(es, f"sP{i}", 2) for i in range(2)]
            if xu is not None:
                ncb = xu
                ntokb = ncb * 8
                sX = mksite(es, "sX", 2)
                hn = [sb(es, f"hnx{i}", [128, D], BF16) for i in range(2)]
                nT = sb(es, "nTx", [128, 8, 1024], BF16)
                X4 = sb(es, "X4x", [128, 32, 8, 16], BF16)
                Ub = sb(es, "Ubx", [128, 32, 128], BF16)
                winb = sb(es, "winbx", [128, 8, 512], BF16)
                E.dma('pool', 'winb', winb[:], W["w_in"].ap().rearrange("(k p) n -> p k n", p=128)[:, :, 1024:1536], writes=['winb'])
            E.dma('pool', 'wd', wd[:], W[wd_name].ap().rearrange("(c p) d -> p c d", p=128), writes=['wd'])
            nsubs_total = ntok // nt
            macros = [(m0, min(4, nsubs_total - m0)) for m0 in range(0, nsubs_total, 4)]
            pend = []

            dnc = [0]

            def flush(upto=None):
                fl = [it for it in pend if upto is None or it[0] <= upto]
                rest = [it for it in pend if not (upto is None or it[0] <= upto)]
                pend.clear(); pend.extend(rest)
                for _, f in fl:
                    f()

            def head_load(mi):
                m0, nsub = macros[mi]
                for st in range(nsub):
                    sl = (m0 + st) % 8
                    tok = (m0 + st) * nt
                    E.dma('sp', f'x{sl}', xsl[sl][0:nt, :], src[tok:tok + nt, :], writes=[f'xsl{sl}'])

            def head_sq(mi, sts):
                m0, nsub = macros[mi]
                for st in sts:
                    if st >= nsub:
                        continue
                    sl = (m0 + st) % 8
                    E.op('act', lambda e: e.activation(out=junk[0:nt, :], in_=xsl[sl][0:nt, :], func=AF.Square,
                                                       accum_out=sH.ss[0:nt, st:st + 1]), reads=[f'xsl{sl}'], writes=['junk', 'sHss'])

            def head_rs(mi):
                m0, nsub = macros[mi]
                rsqrt_site(sH, nt, 0, nsub, 1.0 / D)

            def head_mul(mi, sts):
                m0, nsub = macros[mi]
                for st in sts:
                    if st >= nsub:
                        continue
                    sl = (m0 + st) % 8
                    xb2 = xn[st]; xn2 = f'xn{st}'
                    E.op('act', lambda e: e.mul(out=xb2[0:nt, :], in_=xsl[sl][0:nt, :], mul=sH.rs[0:nt, st:st + 1]),
                         reads=[f'xsl{sl}', 'sHrs'], writes=[xn2])

            def head_pre(mi):
                head_sq(mi, range(4)); head_rs(mi); head_mul(mi, range(4))

            def head_T(mi):
                m0, nsub = macros[mi]
                xb_ = xnT[mi % 2]; xbn = f'xnT{mi % 2}'
                for st in range(nsub):
                    transpose8(xn[st], f'xn{st}', nt, xb_, xbn, st * nt, gcol)

            piece_ctr = [0]

            def gate_up(mi, hooks=None):
                m0, nsub = macros[mi]
                NT = nsub * nt
                xb_ = xnT[mi % 2]; xbn = f'xnT{mi % 2}'
                for pc in range(NPIECE):
                    bsl = piece_ctr[0] % 2; piece_ctr[0] += 1
                    E.dma('sp', f'wg{bsl}', wgb[bsl][:], wg_s[wi][pc], reads=[f"w_ffn{wi + 1}_gate_s"], writes=[f'wgb{bsl}'])
                    E.dma('sp', f'wu{bsl}', wub[bsl][:], wu_s[wi][pc], reads=[f"w_ffn{wi + 1}_up_s"], writes=[f'wub{bsl}'])
                    for f2 in range(2):
                        fc = pc * 2 + f2
                        pg = ps[fc % 2]; pu = ps[2]; ng = PSN[fc % 2]; nu = PSN[2]
                        for k in range(8):
                            E.op('pe', lambda e: e.matmul(pg[:, 0:NT], lhsT=wgb[bsl][:, k, f2 * 128:(f2 + 1) * 128],
                                                          rhs=xb_[:, k, 0:NT], start=(k == 0), stop=(k == 7)),
                                 reads=[f'wgb{bsl}', xbn], writes=[ng])
                        for k in range(8):
                            E.op('pe', lambda e: e.matmul(pu[:, 0:NT], lhsT=wub[bsl][:, k, f2 * 128:(f2 + 1) * 128],
                                                          rhs=xb_[:, k, 0:NT], start=(k == 0), stop=(k == 7)),
                                 reads=[f'wub{bsl}', xbn], writes=[nu])
                        sgb = sg[fc % 2]; sgn_ = f'sg{fc % 2}'
                        E.op('act', lambda e: e.activation(out=sgb[:, 0:NT], in_=pg[:, 0:NT], func=AF.Silu), reads=[ng], writes=[sgn_])
                        E.op('dve', lambda e: e.tensor_tensor(out=hid[:, fc, 0:NT], in0=sgb[:, 0:NT], in1=pu[:, 0:NT], op=ALU.mult),
                             reads=[sgn_, nu], writes=['hid'])
                    if hooks and pc in hooks:
                        hooks[pc]()

            octr = [0]

            def xu_T(o_, on, bpos):
                def f():
                    b = (bpos // nt) % 2
                    E.op('act', lambda e: e.activation(out=junk[0:nt, :], in_=o_[0:nt, :], func=AF.Square,
                                                       accum_out=sX.ss[0:nt, b:b + 1]), reads=[on], writes=['junk', 'sXss'])
                    rsqrt_site(sX, nt, b, b + 1, 1.0 / D)
                    E.op('act', lambda e: e.mul(out=hn[b][0:nt, :], in_=o_[0:nt, :], mul=sX.rs[0:nt, b:b + 1]),
                         reads=[on, 'sXrs'], writes=[f'hnx{b}'])
                    transpose8(hn[b], f'hnx{b}', nt, nT, 'nTx', bpos, gc_mx)
                return f

            def xu_block(blk):
                def f():
                    for j in range(8):
                        b_ = ps[j % 2]
                        for k in range(8):
                            E.op('pe', lambda e: e.matmul(b_[0:ncb, :], lhsT=nT[:, k, j:ntokb:8], rhs=winb[:, k, :],
                                                          start=(k == 0), stop=(k == 7)), reads=['nTx', 'winb'], writes=[PSN[j % 2]])
                        E.op('act', lambda e: e.copy(out=X4[0:ncb, :, j, :], in_=b_[0:ncb, :].rearrange("c (g m) -> c g m", g=32)),
                             reads=[PSN[j % 2]], writes=['X4'])
                    if save_nT:
                        E.dma('pool', 'nts', nt_s[blk][:, :, 0:ntokb], nT[:, :, 0:ntokb], reads=['nTx'], writes=['nt_s'])
                    for g4 in range(8):
                        hf = g4 % 2
                        pb_ = pTs[hf]
                        for gg in range(4):
                            g = g4 * 4 + gg
                            E.op('pe', lambda e: e.transpose(out=pb_[:, gg * 128:gg * 128 + ncb],
                                                             in_=X4[0:ncb, g, :, :].rearrange("c j m -> c (j m)"),
                                                             identity=ident_b[0:ncb, 0:ncb]), reads=['X4', 'ident_b'], writes=[pTn[hf]])
                        E.op('dve', lambda e: e.tensor_copy(out=Ub[:, g4 * 4:(g4 + 1) * 4, 0:ncb],
                                                            in_=pb_[:, 0:512].rearrange("p (g c) -> p g c", g=4)[:, :, 0:ncb]),
                             reads=[pTn[hf]], writes=['Ub'])
                    E.dma('pool', 'ub', u_s[blk][:, :, 0:ncb], Ub[:, :, 0:ncb], reads=['Ub'], writes=['u_s'])
                return f

            def down(mi, st):
                m0, nsub = macros[mi]
                sl = (m0 + st) % 8
                dsl = octr[0] % 2
                b0 = (2 * octr[0]) % 3 if K_ROT else 0; b1 = (2 * octr[0] + 1) % 3 if K_ROT else 1
                pd = [ps[3 + b0], ps[3 + b1]]; pdn = [PSN[3 + b0], PSN[3 + b1]]
                sp_ = sP[dsl]
                xb_ = None
                for half in range(2):
                    for fc in range(NFC):
                        E.op('pe', lambda e: e.matmul(pd[half][0:nt, :], lhsT=hid[:, fc, st * nt:(st + 1) * nt],
                                                      rhs=wd[:, fc, half * 512:(half + 1) * 512], start=(fc == 0), stop=(fc == NFC - 1)),
                             reads=['hid', 'wd'], writes=[pdn[half]])
                dnc[0] += 1
                flush(dnc[0] - 2)
                for half in range(2):
                    E.op('act', lambda e: e.activation(out=junk[0:nt, 0:512], in_=pd[half][0:nt, :], func=AF.Square,
                                                       accum_out=sp_.ss[0:nt, half:half + 1]), reads=[pdn[half]], writes=['junk', sp_.n + 'ss'])
                E.op('dve', lambda e: e.tensor_tensor(out=sp_.ss[0:nt, 0:1], in0=sp_.ss[0:nt, 0:1], in1=sp_.ss[0:nt, 1:2], op=ALU.add),
                     reads=[sp_.n + 'ss'], writes=[sp_.n + 'ss'])
                rsqrt_site(sp_, nt, 0, 1, 1.0 / D)
                o_ = ot[octr[0] % 2]; on = f'ot{octr[0] % 2}'; octr[0] += 1
                for half in range(2):
                    E.op('dve', lambda e: e.scalar_tensor_tensor(out=o_[0:nt, half * 512:(half + 1) * 512], in0=pd[half][0:nt, :],
                                                                 scalar=sp_.rs[0:nt, 0:1], in1=gpost_t[0:nt, half * 512:(half + 1) * 512],
                                                                 op0=ALU.mult, op1=ALU.mult),
                         reads=[pdn[half], sp_.n + 'rs', 'gpost_t'], writes=[on])
                E.op('pool', lambda e: e.tensor_tensor(out=o_[0:nt, :], in0=o_[0:nt, :], in1=xsl[sl][0:nt, :], op=ALU.add),
                     reads=[on, f'xsl{sl}'], writes=[on])
                tok = (m0 + st) * nt
                if dst is not None:
                    E.dma('pool', on, dst[tok:tok + nt, :], o_[0:nt, :], reads=[on], writes=['dst'])
                if xu is not None:
                    bpos = tok % ntokb
                    pend.append((dnc[0], xu_T(o_, on, bpos)))
                    if (tok + nt) % ntokb == 0:
                        pend.append((dnc[0], xu_block(tok // ntokb)))

            head_load(0); head_pre(0); head_T(0)
            for mi in range(len(macros)):
                m0, nsub = macros[mi]
                nxt = mi + 1 < len(macros)
                hooks = {}
                if nxt:
                    hooks = {2: (lambda mi=mi: head_load(mi + 1)),
                             4: (lambda mi=mi: head_sq(mi + 1, (0, 1))), 5: (lambda mi=mi: head_sq(mi + 1, (2, 3))),
                             6: (lambda mi=mi: head_rs(mi + 1)),
                             8: (lambda mi=mi: head_mul(mi + 1, (0, 1))), 9: (lambda mi=mi: head_mul(mi + 1, (2, 3)))}
                gate_up(mi, hooks)
                for st in range(nsub):
                    if nxt and nsub >= 2 and st == 1:
                        pend.append((-1, lambda mi=mi: head_T(mi + 1)))
                    down(mi, st)
                    if nxt and nsub < 2:
                        head_T(mi + 1)
            flush()
            E.barrier()

    def load_norm_T(es_bufs, src, tok0, nt, nst, gcol, nT):
        hsl, hn = es_bufs
        for st in range(nst):
            E.dma('sp', f'h{st % 2}', hsl[st % 2][0:nt, :], src[tok0 + st * nt:tok0 + (st + 1) * nt, :], writes=[f'hsl{st % 2}'])
            E.op('act', lambda e: e.activation(out=junk[0:nt, :], in_=hsl[st % 2][0:nt, :], func=AF.Square,
                                               accum_out=ss[0:nt, 0:1]), reads=[f'hsl{st % 2}'], writes=['junk', 'ss'])
            rsqrt_cols(nt, 1, 1.0 / D)
            E.op('dve', lambda e: e.tensor_scalar(out=hn[0:nt, :], in0=hsl[st % 2][0:nt, :], scalar1=rs[0:nt, 0:1],
                                                  scalar2=None, op0=ALU.mult), reads=[f'hsl{st % 2}', 'rs'], writes=['hn'])
            for k in range(8):
                E.op('pe', lambda e: e.transpose(out=pT[:, k * 128:k * 128 + nt], in_=hn[0:nt, k * 128:(k + 1) * 128],
                                                 identity=ident_b[0:nt, 0:nt]), reads=['hn', 'ident_b'], writes=['pT'])
            E.op('dve', lambda e: e.tensor_tensor(out=nT[:, :, st * nt:(st + 1) * nt],
                                                  in0=pT[:].rearrange("p (k t) -> p k t", k=8)[:, :, 0:nt],
                                                  in1=gcol[:].unsqueeze(2).to_broadcast([128, 8, nt]), op=ALU.mult),
                 reads=['pT'], writes=['nT'])

    def xu_phase(src, nblk, ncb, nt):
        nst = (ncb * 8) // nt
        with ExitStack() as es:
            hsl = [sb(es, f"hsl{i}", [128, D]) for i in range(2)]
            hn = sb(es, "hn", [128, D], BF16)
            nT = sb(es, "nT", [128, 8, 1024], BF16)
            X4 = sb(es, "X4", [128, 32, 8, 16], BF16)
            Ub = sb(es, "Ub", [128, 32, 128], BF16)
            winb = sb(es, "winb", [128, 8, 512], BF16)
            E.dma('pool', 'winb', winb[:], W["w_in"].ap().rearrange("(k p) n -> p k n", p=128)[:, :, 1024:1536], writes=['winb'])
            for blk in range(nblk):
                ntokb = ncb * 8
                load_norm_T((hsl, hn), src, blk * ntokb, nt, nst, gc_mx, nT)
                for j in range(8):
                    b_ = ps[j % 2]
                    for k in range(8):
                        E.op('pe', lambda e: e.matmul(b_[0:ncb, :], lhsT=nT[:, k, j:ntokb:8], rhs=winb[:, k, :],
                                                      start=(k == 0), stop=(k == 7)), reads=['nT', 'winb'], writes=[PSN[j % 2]])
                    E.op('act', lambda e: e.copy(out=X4[0:ncb, :, j, :], in_=b_[0:ncb, :].rearrange("c (g m) -> c g m", g=32)),
                         reads=[PSN[j % 2]], writes=['X4'])
                for g8 in range(4):
                    for gg in range(8):
                        g = g8 * 8 + gg
                        E.op('pe', lambda e: e.transpose(out=pT[:, gg * 128:gg * 128 + ncb],
                                                         in_=X4[0:ncb, g, :, :].rearrange("c j m -> c (j m)"),
                                                         identity=ident_b[0:ncb, 0:ncb]), reads=['X4', 'ident_b'], writes=['pT'])
                    E.op('dve', lambda e: e.tensor_copy(out=Ub[:, g8 * 8:(g8 + 1) * 8, 0:ncb],
                                                        in_=pT[:].rearrange("p (g c) -> p g c", g=8)[:, :, 0:ncb]),
                         reads=['pT'], writes=['Ub'])
                E.dma('pool', 'ub', u_s[blk][:, :, 0:ncb], Ub[:, :, 0:ncb], reads=['Ub'], writes=['u_s'])
            E.barrier()

    def scan_phase(nblk, ncb, Pb, lite=False, blk0=0):
        nmc = nblk * ncb
        with ExitStack() as es:
            S = sb(es, "S", [128, 32, nmc + 1])
            Ub = [sb(es, f"Ubs{i}", [128, 32, 128], BF16) for i in range(2)]
            Wst = sb(es, "Wst", [128, 32, 128], BF16)
            LAM = [sb(es, f"LAM{i}", [128, 32, 128]) for i in range(2)]
            E.dma('sp', 'wst', Wst[:], wst_s[:], reads=['wst_s'], writes=['Wst'])
            E.op('dve', lambda e: e.tensor_copy(out=S[:, :, 0], in_=CAR[:]), reads=['CAR'], writes=[f'S{g}' for g in range(32)])
            for blk in range(nblk):
                ub = Ub[blk % 2]; un = f'Ubs{blk % 2}'
                E.dma('sp', un, ub[:, :, 0:ncb], u_s[blk0 + blk][:, :, 0:ncb], reads=['u_s'], writes=[un])
                for g4 in range(8):
                    b_ = ps[g4 % 4]
                    for gg in range(4):
                        g = g4 * 4 + gg
                        E.op('pe', lambda e: e.matmul(b_[:, gg * 128:gg * 128 + ncb], lhsT=Wst[:, g, :], rhs=ub[:, g, 0:ncb],
                                                      start=True, stop=True), reads=['Wst', un], writes=[PSN[g4 % 4]])
                    E.op('act', lambda e: e.copy(out=S[:, g4 * 4:(g4 + 1) * 4, 1 + blk * ncb:1 + (blk + 1) * ncb],
                                                 in_=b_[:].rearrange("p (g c) -> p g c", g=4)[:, :, 0:ncb]),
                         reads=[PSN[g4 % 4]], writes=[f'S{g4 * 4 + i}' for i in range(4)])
            levels = []
            d = 1
            while d < nmc:
                levels.append((d, 2 * d)); d *= 2
            d = nmc
            while d >= 1:
                levels.append((d, d)); d //= 2
                if lite:
                    break
            for li, (d, t0) in enumerate(levels):
                l = int(np.log2(d))
                cnt = len(range(t0, nmc + 1, 2 * d))
                lm = LAM[li % 2]; ln = f'LAM{li % 2}'
                E.dma('sp', ln, lm[:], lam_s[l], reads=['lam_s'], writes=[ln])
                for g in range(32):
                    b_ = ps[g % 6]
                    tgt = S[:, g, t0:nmc + 1:2 * d]
                    srcv = S[:, g, t0 - d:nmc + 1 - d:2 * d]
                    E.op('pe', lambda e: e.matmul(b_[:, 0:cnt], lhsT=lm[:, g, :], rhs=srcv, start=True, stop=True),
                         reads=[ln, f'S{g}'], writes=[PSN[g % 6]])
                    E.op('dve', lambda e: e.tensor_tensor(out=tgt, in0=b_[:, 0:cnt], in1=tgt, op=ALU.add),
                         reads=[PSN[g % 6], f'S{g}'], writes=[f'S{g}'])
            allS = [f'S{g}' for g in range(32)]
            if not lite:
                E.op('dve', lambda e: e.tensor_copy(out=Pb[:, :, 0:nmc], in_=S[:, :, 0:nmc]), reads=allS, writes=['Pb'])
            E.op('dve', lambda e: e.tensor_copy(out=CAR[:], in_=S[:, :, nmc]), reads=allS, writes=['CAR'])
            E.barrier()

    def lite_scan_merged(nseg, nblk, ncb):
        nmc = nblk * ncb
        GC = 8
        with ExitStack() as es:
            S = [sb(es, f"Sm{i}", [128, GC, nseg, nmc + 1]) for i in range(2)]
            Ub = [sb(es, f"Ubl{i}", [128, GC, 128], BF16) for i in range(2)]
            Wst = sb(es, "Wstl", [128, 32, 128], BF16)
            LAM = [sb(es, f"LAMl{i}", [128, GC, 128]) for i in range(2)]
            E.dma('sp', 'wst', Wst[:], wst_s[:], reads=['wst_s'], writes=['Wstl'])
            uctr = 0; lctr = 0
            for gc in range(32 // GC):
                g0 = gc * GC
                Sc = S[gc % 2]; sn = f'Sm{gc % 2}'
                SN = [f'{sn}g{i}' for i in range(GC)]
                for b in range(nseg * nblk):
                    seg, blk = b // nblk, b % nblk
                    ub = Ub[uctr % 2]; un = f'Ubl{uctr % 2}'; uctr += 1
                    E.dma('sp', un, ub[:, :, 0:ncb], u_s[b][:, g0:g0 + GC, 0:ncb], reads=['u_s'], writes=[un])
                    for q in range(GC // 4):
                        b_ = ps[(b * 2 + q) % 6]; bn = PSN[(b * 2 + q) % 6]
                        for gg in range(4):
                            gi = q * 4 + gg
                            E.op('pe', lambda e: e.matmul(b_[:, gg * 128:gg * 128 + ncb], lhsT=Wst[:, g0 + gi, :], rhs=ub[:, gi, 0:ncb],
                                                          start=True, stop=True), reads=['Wstl', un], writes=[bn])
                        E.op('act', lambda e: e.copy(out=Sc[:, q * 4:(q + 1) * 4, seg, 1 + blk * ncb:1 + (blk + 1) * ncb],
                                                     in_=b_[:].rearrange("p (g c) -> p g c", g=4)[:, :, 0:ncb]),
                             reads=[bn], writes=SN[q * 4:(q + 1) * 4])
                d = 1
                while d < nmc:
                    l = int(np.log2(d)); t0 = 2 * d
                    cnt = len(range(t0, nmc + 1, 2 * d))
                    lm = LAM[lctr % 2]; ln = f'LAMl{lctr % 2}'; lctr += 1
                    E.dma('sp', ln, lm[:], lam_s[l][:, g0:g0 + GC, :], reads=['lam_s'], writes=[ln])
                    for gi in range(GC):
                        if nseg * cnt <= 512:
                            parts = [(0, nseg)]
                        else:
                            parts = [(sg_, sg_ + 1) for sg_ in range(nseg)]
                        for pi, (s0, s1) in enumerate(parts):
                            ns_ = s1 - s0
                            bi = (gi * 3 + pi) % 6
                            b_ = ps[bi]; bn = PSN[bi]
                            tgt = Sc[:, gi, s0:s1, t0:nmc + 1:2 * d]
                            srcv = Sc[:, gi, s0:s1, t0 - d:nmc + 1 - d:2 * d]
                            E.op('pe', lambda e: e.matmul(b_[:, 0:ns_ * cnt], lhsT=lm[:, gi, :], rhs=srcv, start=True, stop=True),
                                 reads=[ln, SN[gi]], writes=[bn])
                            E.op('dve', lambda e: e.tensor_tensor(out=tgt, in0=b_[:, 0:ns_ * cnt].rearrange("p (s c) -> p s c", s=ns_),
                                                                  in1=tgt, op=ALU.add), reads=[bn, SN[gi]], writes=[SN[gi]])
                    d *= 2
                l = int(np.log2(nmc))
                lm = LAM[lctr % 2]; ln = f'LAMl{lctr % 2}'; lctr += 1
                E.dma('sp', ln, lm[:], lam_s[l][:, g0:g0 + GC, :], reads=['lam_s'], writes=[ln])
                E.op('dve', lambda e: e.tensor_copy(out=Sc[:, :, 0, 0], in_=CAR[:, g0:g0 + GC]), reads=['CAR'], writes=SN)
                for seg in range(nseg):
                    for gi in range(GC):
                        bi = gi % 6
                        b_ = ps[bi]; bn = PSN[bi]
                        E.op('pe', lambda e: e.matmul(b_[:, 0:1], lhsT=lm[:, gi, :], rhs=Sc[:, gi, seg, 0:1], start=True, stop=True),
                             reads=[ln, SN[gi]], writes=[bn])
                        if seg + 1 < nseg:
                            dstc = Sc[:, gi, seg + 1, 0:1]
                            E.op('dve', lambda e: e.tensor_tensor(out=dstc, in0=b_[:, 0:1], in1=Sc[:, gi, seg, nmc:nmc + 1], op=ALU.add),
                                 reads=[bn, SN[gi]], writes=[SN[gi]])
                        else:
                            E.op('dve', lambda e: e.tensor_tensor(out=CAR[:, g0 + gi:g0 + gi + 1], in0=b_[:, 0:1],
                                                                  in1=Sc[:, gi, seg, nmc:nmc + 1], op=ALU.add),
                                 reads=[bn, SN[gi]], writes=['CAR'])
            E.barrier()

    def mixer_phase(src, dst, nblk, ncb, nt, Pb, vs_out=None):
        nst = (ncb * 8) // nt
        ntokb = ncb * 8
        with ExitStack() as es:
            hsl = [sb(es, f"hsl{i}", [128, D]) for i in range(2)]
            hn = [sb(es, f"hn{i}", [128, D], BF16) for i in range(2)]
            nT = sb(es, "nT", [128, 8, 1024], BF16)
            Ub = sb(es, "Ubm", [128, 32, 128], BF16)
            Kt = sb(es, "Kt", [128, 32, 128], BF16); Cm = sb(es, "Cm", [128, 32, 128], BF16)
            wina = sb(es, "wina", [128, 8, 1024], BF16); wout = sb(es, "wout", [128, 8, D], BF16)
            wglu = sb(es, "wglu", [128, 4, 512], BF16)
            mixT = sb(es, "mixT", [128, 8, 1024], BF16)
            Y = sb(es, "Y", [128, 8, 512])
            ybf = [sb(es, f"ybf{i}", [128, 512], BF16) for i in range(2)]
            y2T = [sb(es, f"y2T{i}", [128, 4, 128], BF16) for i in range(2)]
            gt = [sb(es, f"gt{i}", [128, 512]) for i in range(2)]
            uu = [sb(es, f"uu{i}", [128, 512]) for i in range(2)]; vv = [sb(es, f"vv{i}", [128, 512]) for i in range(2)]
            vnf = [sb(es, f"vnf{i}", [128, 512]) for i in range(2)]; vnb = [sb(es, f"vnb{i}", [128, 512], BF16) for i in range(2)]
            aa = [sb(es, f"aa{i}", [128, 512]) for i in range(2)]; anb = [sb(es, f"anb{i}", [128, 512], BF16) for i in range(2)]
            ot = [sb(es, f"otm{i}", [128, D]) for i in range(2)]
            sL = [mksite(es, f"sL{i}", 1) for i in range(2)]; sB = mksite(es, "sB", 8)
            sV = [mksite(es, f"sV{i}", 4) for i in range(2)]; sA = [mksite(es, f"sA{i}", 1) for i in range(2)]
            sW = [mksite(es, f"sW{i}", 2) for i in range(2)]
            gpostm = sb(es, "gpostm", [128, D]); gv_t = sb(es, "gv_t", [128, 512]); bglu_t = sb(es, "bglu_t", [128, 512])
            E.dma('sp', 'gpost', gpostm[:], W["g_mix_post"].ap().partition_broadcast(128), writes=['gpostm'])
            E.dma('sp', 'gpost', gv_t[:], W["gmlp_g_v"].ap().partition_broadcast(128), writes=['gv_t'])
            E.dma('sp', 'gpost', bglu_t[:], W["s5_b_glu"].ap().partition_broadcast(128), writes=['bglu_t'])
            E.dma('sp', 'kt', Kt[:], kt_s[:], reads=['kt_s'], writes=['Kt'])
            E.dma('sp', 'cm', Cm[:], cm_s[:], reads=['cm_s'], writes=['Cm'])
            winv = W["w_in"].ap().rearrange("(k p) n -> p k n", p=128)
            E.dma('pool', 'wina', wina[:], winv[:, :, 0:1024], writes=['wina'])
            E.dma('pool', 'wout', wout[:], W["w_out"].ap().rearrange("(k p) n -> p k n", p=128), writes=['wout'])
            E.dma('pool', 'wglu', wglu[:], W["s5_w_glu"].ap().rearrange("(k p) n -> p k n", p=128), writes=['wglu'])
            oi = 0
            for blk in range(nblk):
                tokb = blk * ntokb
                E.dma('sp', 'ubm', Ub[:, :, 0:ncb], u_s[blk][:, :, 0:ncb], reads=['u_s'], writes=['Ubm'])
                E.dma('sp', 'ntl', nT[:, :, 0:ntokb], nt_s[blk][:, :, 0:ntokb], reads=['nt_s'], writes=['nT'])
                for g4 in range(8):
                    b_ = ps[g4 % 2]
                    for gg in range(4):
                        g = g4 * 4 + gg
                        E.op('pe', lambda e: e.matmul(b_[0:ncb, gg * 128:(gg + 1) * 128], lhsT=Ub[:, g, 0:ncb], rhs=Kt[:, g, :],
                                                      start=True, stop=False), reads=['Ubm', 'Kt'], writes=[PSN[g4 % 2]])
                        E.op('pe', lambda e: e.matmul(b_[0:ncb, gg * 128:(gg + 1) * 128], lhsT=Pb[:, g, blk * ncb:(blk + 1) * ncb],
                                                      rhs=Cm[:, g, :], start=False, stop=True), reads=['Pb', 'Cm'], writes=[PSN[g4 % 2]])
                    E.op('act', lambda e: e.activation(
                        out=Y[0:ncb].rearrange("c t (g n) -> c t g n", g=32)[:, :, g4 * 4:(g4 + 1) * 4, :],
                        in_=b_[0:ncb, :].rearrange("c (g t n) -> c t g n", g=4, t=8), func=AF.Gelu_apprx_tanh),
                        reads=[PSN[g4 % 2]], writes=[f'Yg{g4}'] + [f'Yj{j}' for j in range(8)])
                Yall = [f'Yg{i}' for i in range(8)]
                def glu_A(j):
                    p = j % 2
                    pb_ = pTs[p]; pbn = pTn[p]
                    E.op('dve', lambda e: e.tensor_copy(out=ybf[p][0:ncb, :], in_=Y[0:ncb, j, :]), reads=Yall + [f'Yj{j}'], writes=[f'ybf{p}'])
                    for kt in range(4):
                        E.op('pe', lambda e: e.transpose(out=pb_[:, kt * 128:kt * 128 + ncb], in_=ybf[p][0:ncb, kt * 128:(kt + 1) * 128],
                                                         identity=ident_b[0:ncb, 0:ncb]), reads=[f'ybf{p}', 'ident_b'], writes=[pbn])
                    E.op('act', lambda e: e.copy(out=y2T[p][:, :, 0:ncb], in_=pb_[:, 0:512].rearrange("p (k c) -> p k c", k=4)[:, :, 0:ncb]),
                         reads=[pbn], writes=[f'y2T{p}'])

                def glu_B(j):
                    p = j % 2
                    for kt in range(4):
                        E.op('pe', lambda e: e.matmul(ps[2 + p][0:ncb, :], lhsT=y2T[p][:, kt, 0:ncb], rhs=wglu[:, kt, :],
                                                      start=(kt == 0), stop=(kt == 3)), reads=[f'y2T{p}', 'wglu'], writes=[PSN[2 + p]])
                    E.op('dve', lambda e: e.tensor_tensor(out=gt[p][0:ncb, :], in0=ps[2 + p][0:ncb, :], in1=bglu_t[0:ncb, :], op=ALU.add),
                         reads=[PSN[2 + p], 'bglu_t'], writes=[f'gt{p}'])
                    E.op('act', lambda e: e.activation(out=gt[p][0:ncb, :], in_=gt[p][0:ncb, :], func=AF.Tanh, scale=0.5),
                         reads=[f'gt{p}'], writes=[f'gt{p}'])
                    E.op('dve', lambda e: e.scalar_tensor_tensor(out=Y[0:ncb, j, :], in0=gt[p][0:ncb, :], scalar=1.0, in1=Y[0:ncb, j, :],
                                                                 op0=ALU.add, op1=ALU.mult), reads=[f'gt{p}', f'Yj{j}'] + Yall, writes=[f'Yj{j}'])
                    E.op('act', lambda e: e.activation(out=junk[0:ncb, 0:512], in_=Y[0:ncb, j, :], func=AF.Square,
                                                       accum_out=sB.ss[0:ncb, j:j + 1]), reads=[f'Yj{j}'], writes=['junk', 'sBss'])
                for i in range(9):
                    if i < 8:
                        glu_A(i)
                    if i >= 1:
                        glu_B(i - 1)
                rsqrt_site(sB, ncb, 0, 8, 1.0 / (4 * 512))
                for j in range(8):
                    p = j % 2
                    pb_ = pTs[p]; pbn = pTn[p]
                    E.op('act', lambda e: e.mul(out=ybf[p][0:ncb, :], in_=Y[0:ncb, j, :], mul=sB.rs[0:ncb, j:j + 1]),
                         reads=[f'Yj{j}', 'sBrs'], writes=[f'ybf{p}'])
                    for kt in range(4):
                        E.op('pe', lambda e: e.transpose(out=pb_[:, kt * 128:kt * 128 + ncb], in_=ybf[p][0:ncb, kt * 128:(kt + 1) * 128],
                                                         identity=ident_b[0:ncb, 0:ncb]), reads=[f'ybf{p}', 'ident_b'], writes=[pbn])
                    E.op('dve', lambda e: e.tensor_tensor(out=mixT[:, 4:8, j:ntokb:8],
                                                          in0=pb_[:, 0:512].rearrange("p (k c) -> p k c", k=4)[:, :, 0:ncb],
                                                          in1=gc_b[:].unsqueeze(2).to_broadcast([128, 4, ncb]), op=ALU.mult),
                         reads=[pbn], writes=['mixTb'])
                def gm_A(st):
                    p = st % 2
                    tsl = slice(st * nt, (st + 1) * nt)
                    pz = [ps[2 * p], ps[2 * p + 1]]; pzn = [PSN[2 * p], PSN[2 * p + 1]]
                    for half in range(2):
                        for k in range(8):
                            E.op('pe', lambda e: e.matmul(pz[half][0:nt, :], lhsT=nT[:, k, tsl], rhs=wina[:, k, half * 512:(half + 1) * 512],
                                                          start=(k == 0), stop=(k == 7)), reads=['nT', 'wina'], writes=[pzn[half]])
                    E.op('act', lambda e: e.activation(out=uu[p][0:nt, :], in_=pz[0][0:nt, :], func=AF.Gelu_apprx_tanh), reads=[pzn[0]], writes=[f'uu{p}'])
                    E.op('act', lambda e: e.activation(out=vv[p][0:nt, :], in_=pz[1][0:nt, :], func=AF.Gelu_apprx_tanh), reads=[pzn[1]], writes=[f'vv{p}'])
                    for h in range(4):
                        E.op('act', lambda e: e.activation(out=junk[0:nt, 0:128], in_=vv[p][0:nt, h * 128:(h + 1) * 128], func=AF.Square,
                                                           accum_out=sV[p].ss[0:nt, h:h + 1]), reads=[f'vv{p}'], writes=['junk', sV[p].n + 'ss'])
                    rsqrt_site(sV[p], nt, 0, 4, 1.0 / 128)
                    E.op('dve', lambda e: e.tensor_tensor(out=vnf[p][0:nt, :].rearrange("t (h d) -> t h d", h=4),
                                                          in0=vv[p][0:nt, :].rearrange("t (h d) -> t h d", h=4),
                                                          in1=sV[p].rs[0:nt, 0:4].unsqueeze(2).to_broadcast([nt, 4, 128]), op=ALU.mult),
                         reads=[f'vv{p}', sV[p].n + 'rs'], writes=[f'vnf{p}'])
                    E.op('dve', lambda e: e.tensor_tensor(out=vnf[p][0:nt, :], in0=vnf[p][0:nt, :], in1=gv_t[0:nt, :], op=ALU.mult),
                         reads=[f'vnf{p}', 'gv_t'], writes=[f'vnf{p}'])
                    E.op('act', lambda e: e.copy(out=vnb[p][0:nt, :], in_=vnf[p][0:nt, :]), reads=[f'vnf{p}'], writes=[f'vnb{p}'])
                    if vs_out is not None:
                        E.dma('pool', 'vs', vs_out[tokb + st * nt:tokb + (st + 1) * nt, :], vnf[p][0:nt, :], reads=[f'vnf{p}'], writes=['vs_out'])

                def gm_B(st):
                    p = st % 2
                    psp = ps[4 + p]; pspn = PSN[4 + p]
                    for h in range(4):
                        E.op('pe', lambda e: e.matmul(psp[0:nt, h * 128:(h + 1) * 128], lhsT=wsT[0:nt, h, 0:nt],
                                                      rhs=vnb[p][0:nt, h * 128:(h + 1) * 128], start=True, stop=True),
                             reads=['wsT', f'vnb{p}'], writes=[pspn])
                    for h in range(4):
                        E.op('dve', lambda e: e.scalar_tensor_tensor(out=aa[p][0:nt, h * 128:(h + 1) * 128], in0=psp[0:nt, h * 128:(h + 1) * 128],
                                                                     scalar=bsT[0:nt, h:h + 1], in1=uu[p][0:nt, h * 128:(h + 1) * 128],
                                                                     op0=ALU.add, op1=ALU.mult), reads=[pspn, f'uu{p}'], writes=[f'aa{p}'])
                    E.op('act', lambda e: e.activation(out=junk[0:nt, 0:512], in_=aa[p][0:nt, :], func=AF.Square,
                                                       accum_out=sA[p].ss[0:nt, 0:1]), reads=[f'aa{p}'], writes=['junk', sA[p].n + 'ss'])
                    rsqrt_site(sA[p], nt, 0, 1, 1.0 / 512)
                    E.op('act', lambda e: e.mul(out=anb[p][0:nt, :], in_=aa[p][0:nt, :], mul=sA[p].rs[0:nt, 0:1]),
                         reads=[f'aa{p}', sA[p].n + 'rs'], writes=[f'anb{p}'])

                def gm_C(st):
                    p = st % 2
                    tsl = slice(st * nt, (st + 1) * nt)
                    pb_ = pTs[p]; pbn = pTn[p]
                    for kt in range(4):
                        E.op('pe', lambda e: e.transpose(out=pb_[:, kt * 128:kt * 128 + nt], in_=anb[p][0:nt, kt * 128:(kt + 1) * 128],
                                                         identity=ident_b[0:nt, 0:nt]), reads=[f'anb{p}', 'ident_b'], writes=[pbn])
                    E.op('dve', lambda e: e.tensor_tensor(out=mixT[:, 0:4, tsl],
                                                          in0=pb_[:, 0:512].rearrange("p (k c) -> p k c", k=4)[:, :, 0:nt],
                                                          in1=gc_a[:].unsqueeze(2).to_broadcast([128, 4, nt]), op=ALU.mult),
                         reads=[pbn], writes=['mixTa'])
                for i in range(nst + 2):
                    if i < nst:
                        gm_A(i)
                    if 0 <= i - 1 < nst:
                        gm_B(i - 1)
                    if 0 <= i - 2 < nst:
                        gm_C(i - 2)
                for st in range(nst):
                    p = st % 2
                    tsl = slice(st * nt, (st + 1) * nt)
                    hs_ = hsl[p]; hnm = f'hsl{p}'
                    E.dma('sp', f'h{p}', hs_[0:nt, :], src[tokb + st * nt:tokb + (st + 1) * nt, :], writes=[hnm])
                    pw = [ps[2 * p], ps[2 * p + 1]]; pwn = [PSN[2 * p], PSN[2 * p + 1]]
                    for half in range(2):
                        for k in range(8):
                            E.op('pe', lambda e: e.matmul(pw[half][0:nt, :], lhsT=mixT[:, k, tsl], rhs=wout[:, k, half * 512:(half + 1) * 512],
                                                          start=(k == 0), stop=(k == 7)), reads=['mixTa', 'mixTb', 'wout'], writes=[pwn[half]])
                        E.op('act', lambda e: e.activation(out=junk[0:nt, 0:512], in_=pw[half][0:nt, :], func=AF.Square,
                                                           accum_out=sW[p].ss[0:nt, half:half + 1]), reads=[pwn[half]], writes=['junk', sW[p].n + 'ss'])
                    E.op('dve', lambda e: e.tensor_tensor(out=sW[p].ss[0:nt, 0:1], in0=sW[p].ss[0:nt, 0:1], in1=sW[p].ss[0:nt, 1:2], op=ALU.add),
                         reads=[sW[p].n + 'ss'], writes=[sW[p].n + 'ss'])
                    rsqrt_site(sW[p], nt, 0, 1, 1.0 / D)
                    o_ = ot[oi % 2]; on = f'otm{oi % 2}'; oi += 1
                    for half in range(2):
                        E.op('dve', lambda e: e.scalar_tensor_tensor(out=o_[0:nt, half * 512:(half + 1) * 512], in0=pw[half][0:nt, :],
                                                                     scalar=sW[p].rs[0:nt, 0:1], in1=gpostm[0:nt, half * 512:(half + 1) * 512],
                                                                     op0=ALU.mult, op1=ALU.mult), reads=[pwn[half], sW[p].n + 'rs', 'gpostm'], writes=[on])
                    E.op('dve', lambda e: e.tensor_tensor(out=o_[0:nt, :], in0=o_[0:nt, :], in1=hs_[0:nt, :], op=ALU.add),
                         reads=[on, hnm], writes=[on])
                    E.dma('pool', on, dst[tokb + st * nt:tokb + (st + 1) * nt, :], o_[0:nt, :], reads=[on], writes=['dst'])
            E.barrier()

    def state_out(o_re, o_im):
        E.dma('pool', 'sto', o_re.ap().rearrange("g p -> p g"), CAR[0:64, :], reads=['CAR'], writes=['sto'], **NCD)
        E.dma('pool', 'sto', o_im.ap().rearrange("g p -> p g"), CAR[64:128, :], reads=['CAR'], writes=['sto'], **NCD)

    def run_segment(src, dst, ntok, nt, nblk, ncb, vs_out=None, lite=False):
        ffn_phase(src, h_s.ap(), ntok, nt, 0, "w_ffn1_down", gc_f1, "g_ffn1_post", xu=ncb, save_nT=not lite)
        if lite:
            scan_phase(nblk, ncb, None, lite=True)
            return
        with ExitStack() as es:
            Pb = sb(es, "Pb", [128, 32, NMC], BF16)
            scan_phase(nblk, ncb, Pb)
            mixer_phase(h_s.ap(), h2_s.ap(), nblk, ncb, nt, Pb, vs_out)
        ffn_phase(h2_s.ap(), dst, ntok, nt, 1, "w_ffn2_down", gc_f2, "g_ffn2_post")

    E.op('dve', lambda e: e.memset(CAR[:], 0.0), writes=['CAR'])
    if NPRE > 0:
        ffn_phase(xpre.ap(), None, NPRE * SEG, 128, 0, "w_ffn1_down", gc_f1, "g_ffn1_post", xu=128)
        lite_scan_merged(NPRE, NBLK, 128)
    for sg in range(NSEG):
        run_segment(xp.ap()[sg * SEG:(sg + 1) * SEG, :], yp.ap()[sg * SEG:(sg + 1) * SEG, :], SEG, 128, NBLK, 128)
    state_out(sp_re, sp_im)
    E.barrier()
    E.dma('sp', 'setup', CAR[0:64, :], st_re.ap().rearrange("g p -> p g"), writes=['CAR'], **NCD)
    E.dma('sp', 'setup', CAR[64:128, :], st_im.ap().rearrange("g p -> p g"), writes=['CAR'], **NCD)
    run_segment(xs.ap(), ys.ap(), 16, 16, 1, 2, vs_out=vs.ap())
    state_out(ss_re, ss_im)
    E.barrier()
    top.close()
    return nc


_WNAMES = ["g_ffn1_pre", "w_ffn1_gate", "w_ffn1_up", "w_ffn1_down", "g_ffn1_post", "g_mix_pre", "w_in", "gmlp_g_v",
           "gmlp_w_s", "gmlp_b_s", "s5_lam_re", "s5_lam_im", "s5_log_dt", "s5_b_re", "s5_b_im", "s5_c_re", "s5_c_im",
           "s5_d", "s5_w_glu", "s5_b_glu", "g_a_out", "g_b_out", "w_out", "g_mix_post", "g_ffn2_pre", "w_ffn2_gate",
           "w_ffn2_up", "w_ffn2_down", "g_ffn2_post"]


def kernel(**inputs):
    x_prompt = np.asarray(inputs["x_prompt"], dtype=np.float32)
    x_sample = np.asarray(inputs["x_sample"], dtype=np.float32)
    B, LT, _ = x_prompt.shape
    CPS = 8 // B
    L = LT // CPS
    NBLK = min(4, L // 1024)
    LPRE = (CPS - 1) * L
    nc = build_nc(L, NBLK, LPRE)
    wmap = {nm: np.ascontiguousarray(np.asarray(inputs[nm], dtype=np.float32)[0]) for nm in _WNAMES}
    sre = np.asarray(inputs["state_ssm_re"], dtype=np.float32)[0]
    sim = np.asarray(inputs["state_ssm_im"], dtype=np.float32)[0]
    in_maps = []
    for c in range(8):
        m = dict(wmap)
        sq, i = c // CPS, c % CPS
        m["xp"] = np.ascontiguousarray(x_prompt[sq, i * L:(i + 1) * L])
        pre = np.zeros((max(LPRE, 1), D), np.float32)
        if i > 0:
            pre[LPRE - i * L:] = x_prompt[sq, 0:i * L]
        m["xpre"] = pre
        m["xs"] = np.ascontiguousarray(x_sample[c])
        m["st_re"] = np.ascontiguousarray(sre[c]); m["st_im"] = np.ascontiguousarray(sim[c])
        in_maps.append(m)
    res = run_bass_kernel_spmd(nc, in_maps, core_ids=list(range(8)))
    R = res.results
    y_p = np.stack([np.concatenate([R[sq * CPS + i]["yp"] for i in range(CPS)], axis=0) for sq in range(B)])
    y_s = np.stack([R[c]["ys"] for c in range(8)])
    spr = np.stack([R[sq * CPS + CPS - 1]["sp_re"] for sq in range(B)])[None]
    spi = np.stack([R[sq * CPS + CPS - 1]["sp_im"] for sq in range(B)])[None]
    ssr = np.stack([R[c]["ss_re"] for c in range(8)])[None]
    ssi = np.stack([R[c]["ss_im"] for c in range(8)])[None]
    v_s = np.stack([R[c]["vs"] for c in range(8)])[None]
    return (y_p.astype(np.float32), y_s.astype(np.float32), spr.astype(np.float32), spi.astype(np.float32),
            ssr.astype(np.float32), ssi.astype(np.float32), v_s.astype(np.float32))
```
